# Optimizing a Trainium2 kernel written in Bass

```python
import jax, jax.numpy as jnp
from jax import lax
import numpy as np

D_MODEL = 4096
BATCH = 4
SEQ = 4096
DEPTH = 1

HEAD_DIM = 128
N_HEADS_NA = D_MODEL // (2 * HEAD_DIM)
N_HEADS_DIL = D_MODEL // (2 * HEAD_DIM)
D_NA = N_HEADS_NA * HEAD_DIM
D_DIL = N_HEADS_DIL * HEAD_DIM
D_MIX = D_NA + D_DIL
GRID_W = 64
NA_ROWS = 8
NA_COLS = 16
DIL_PAIRS = ((128, 1), (512, 4), (2048, 16))
DIL_BLOCK = 64
D_FF = ((8 * D_MODEL // 3 + 255) // 256) * 256
CONV_W = 3
EPS = 1e-6
NEG_INF = -1e30

kernel_name = 'hybrid_natten_dilated_convffn_block'


def rms_norm(x, g):
    xf = x.astype(jnp.float32)
    y = xf * lax.rsqrt(jnp.mean(xf * xf, axis=-1, keepdims=True) + EPS)
    return (y * g.astype(jnp.float32)).astype(x.dtype)


def split_heads(t, n):
    b, s, _ = t.shape
    return t.reshape(b, s, n, HEAD_DIM).transpose(0, 2, 1, 3)


def merge_heads(t):
    b, h, s, d = t.shape
    return t.transpose(0, 2, 1, 3).reshape(b, s, h * d)


def neighbourhood_attention(q, k, v, rel_bias):
    b, h, s, hd = q.shape
    rows = s // GRID_W
    kh = min(NA_ROWS, rows)
    qg = q.reshape(b, h, rows, GRID_W, hd)
    kg = k.reshape(b, h, rows, GRID_W, hd)
    vg = v.reshape(b, h, rows, GRID_W, hd)
    col = jnp.arange(GRID_W)
    c0 = jnp.clip(col - NA_COLS // 2, 0, GRID_W - NA_COLS)
    col_in = (col[None, :] >= c0[:, None]) & (col[None, :] < c0[:, None] + NA_COLS)
    dc_idx = jnp.clip(col[None, :] - col[:, None] + NA_COLS - 1, 0, 2 * NA_COLS - 2)
    col_bias = rel_bias[:, :, dc_idx].astype(jnp.float32)

    def one_row(r):
        r0 = jnp.clip(r - kh // 2, 0, rows - kh)
        q_r = lax.dynamic_index_in_dim(qg, r, axis=2, keepdims=False)
        k_r = lax.dynamic_slice_in_dim(kg, r0, kh, axis=2)
        v_r = lax.dynamic_slice_in_dim(vg, r0, kh, axis=2)
        dr_idx = r0 + jnp.arange(kh) - r + NA_ROWS - 1
        bias = jnp.take(col_bias, dr_idx, axis=1).transpose(0, 2, 1, 3)
        sc = jnp.einsum('bhqc,bhikc->bhqik', q_r, k_r).astype(jnp.float32) + bias[None]
        sc = jnp.where(col_in[:, None, :], sc, NEG_INF)
        p = jax.nn.softmax(sc.reshape(b, h, GRID_W, kh * GRID_W), axis=-1)
        p = p.reshape(b, h, GRID_W, kh, GRID_W).astype(v.dtype)
        return jnp.einsum('bhqik,bhikc->bhqc', p, v_r)

    out = lax.map(one_row, jnp.arange(rows))
    return out.transpose(1, 2, 0, 3, 4).reshape(b, h, s, hd)


def dilated_branch(q, k, v, window, dil, slopes):
    b, h, s, hd = q.shape
    half = window // (2 * dil)
    L = s // dil
    nb = -(-L // DIL_BLOCK)
    Lp = nb * DIL_BLOCK
    kb_len = DIL_BLOCK + 2 * half

    def to_res(t):
        return t.reshape(b, h, L, dil, hd).transpose(0, 1, 3, 2, 4)

    qr = jnp.pad(to_res(q), ((0, 0), (0, 0), (0, 0), (0, Lp - L), (0, 0)))
    kr = jnp.pad(to_res(k), ((0, 0), (0, 0), (0, 0), (half, Lp - L + half), (0, 0)))
    vr = jnp.pad(to_res(v), ((0, 0), (0, 0), (0, 0), (half, Lp - L + half), (0, 0)))
    qb = qr.reshape(b, h, dil, nb, DIL_BLOCK, hd)
    key_idx = (jnp.arange(nb) * DIL_BLOCK)[:, None] + jnp.arange(kb_len)[None, :]
    kb = kr[:, :, :, key_idx]
    vb = vr[:, :, :, key_idx]
    rel = jnp.arange(kb_len)[None, :] - half - jnp.arange(DIL_BLOCK)[:, None]
    kpos = (key_idx - half)[:, None, :]
    valid = (jnp.abs(rel) <= half)[None] & (kpos >= 0) & (kpos < L)
    dist = (dil * jnp.abs(rel)).astype(jnp.float32)
    sc = jnp.einsum('bhrnqc,bhrnkc->bhrnqk', qb, kb).astype(jnp.float32)
    sc = sc - slopes[None, :, None, None, None, None] * dist
    sc = jnp.where(valid, sc, NEG_INF)
    m = jnp.max(sc, axis=-1, keepdims=True)
    p = jnp.exp(sc - m)
    l = jnp.sum(p, axis=-1, keepdims=True)
    o = jnp.einsum('bhrnqk,bhrnkc->bhrnqc', p.astype(v.dtype), vb).astype(jnp.float32) / l
    lse = (m + jnp.log(l))[..., 0]
    o = o.reshape(b, h, dil, Lp, hd)[:, :, :, :L].transpose(0, 1, 3, 2, 4).reshape(b, h, s, hd)
    lse = lse.reshape(b, h, dil, Lp)[:, :, :, :L].transpose(0, 1, 3, 2).reshape(b, h, s)
    return o, lse


def dilated_attention(q, k, v, slopes):
    outs, lses = [], []
    for window, dil in DIL_PAIRS:
        o, lse = dilated_branch(q, k, v, window, dil, slopes)
        outs.append(o)
        lses.append(lse)
    w = jax.nn.softmax(jnp.stack(lses, axis=0), axis=0)
    out = jnp.einsum('nbhs,nbhsc->bhsc', w, jnp.stack(outs, axis=0))
    return out.astype(q.dtype)


def setup_inputs(seed: int = 0) -> dict:
    key = jax.random.key(seed)
    ks = jax.random.split(key, 16)
    f32 = jnp.float32
    nrm = lambda k, shape, sc: jax.random.normal(k, shape, f32) * sc
    x = jax.random.normal(ks[0], (BATCH, SEQ, D_MODEL), f32)
    norm1_g = 1.0 + nrm(ks[1], (DEPTH, D_MODEL), 0.02)
    w_in = nrm(ks[2], (DEPTH, D_MODEL, 3 * D_MIX), D_MODEL ** -0.5)
    qn_na = 1.0 + nrm(ks[3], (DEPTH, HEAD_DIM), 0.02)
    kn_na = 1.0 + nrm(ks[4], (DEPTH, HEAD_DIM), 0.02)
    qn_dil = 1.0 + nrm(ks[5], (DEPTH, HEAD_DIM), 0.02)
    kn_dil = 1.0 + nrm(ks[6], (DEPTH, HEAD_DIM), 0.02)
    rel_bias = nrm(ks[7], (DEPTH, N_HEADS_NA, 2 * NA_ROWS - 1, 2 * NA_COLS - 1), 0.5)
    out_norm_g = 1.0 + nrm(ks[8], (DEPTH, D_MIX), 0.02)
    w_out = nrm(ks[9], (DEPTH, D_MIX, D_MODEL), D_MIX ** -0.5)
    norm2_g = 1.0 + nrm(ks[10], (DEPTH, D_MODEL), 0.02)
    w_up = nrm(ks[11], (DEPTH, D_MODEL, 2 * D_FF), D_MODEL ** -0.5)
    conv_w = nrm(ks[12], (DEPTH, CONV_W, 2 * D_FF), CONV_W ** -0.5)
    conv_b = nrm(ks[13], (DEPTH, 2 * D_FF), 0.02)
    w_down = nrm(ks[14], (DEPTH, D_FF, D_MODEL), D_FF ** -0.5)
    return {'x': x, 'norm1_g': norm1_g, 'w_in': w_in, 'qn_na': qn_na, 'kn_na': kn_na,
            'qn_dil': qn_dil, 'kn_dil': kn_dil, 'rel_bias': rel_bias, 'out_norm_g': out_norm_g,
            'w_out': w_out, 'norm2_g': norm2_g, 'w_up': w_up, 'conv_w': conv_w,
            'conv_b': conv_b, 'w_down': w_down}


def reference(x, norm1_g, w_in, qn_na, kn_na, qn_dil, kn_dil, rel_bias, out_norm_g,
              w_out, norm2_g, w_up, conv_w, conv_b, w_down):
    scale = HEAD_DIM ** -0.5
    slopes = jnp.exp2(-8.0 * jnp.arange(1, N_HEADS_DIL + 1, dtype=jnp.float32) / N_HEADS_DIL)
    splits = [D_NA, 2 * D_NA, 3 * D_NA, 3 * D_NA + D_DIL, 3 * D_NA + 2 * D_DIL]
    for l in range(DEPTH):
        h = rms_norm(x, norm1_g[l])
        proj = h @ w_in[l]
        qa, ka, va, qd, kd, vd = jnp.split(proj, splits, axis=-1)
        qa = rms_norm(split_heads(qa, N_HEADS_NA), qn_na[l]) * scale
        ka = rms_norm(split_heads(ka, N_HEADS_NA), kn_na[l])
        va = split_heads(va, N_HEADS_NA)
        qd = rms_norm(split_heads(qd, N_HEADS_DIL), qn_dil[l]) * scale
        kd = rms_norm(split_heads(kd, N_HEADS_DIL), kn_dil[l])
        vd = split_heads(vd, N_HEADS_DIL)
        out_na = merge_heads(neighbourhood_attention(qa, ka, va, rel_bias[l]))
        out_dil = merge_heads(dilated_attention(qd, kd, vd, slopes))
        mix = jnp.concatenate([rms_norm(out_na, out_norm_g[l, :D_NA]),
                               rms_norm(out_dil, out_norm_g[l, D_NA:])], axis=-1)
        x = x + mix @ w_out[l]
        h2 = rms_norm(x, norm2_g[l])
        u = h2 @ w_up[l]
        u = lax.conv_general_dilated(
            u, conv_w[l].astype(u.dtype)[:, None, :], window_strides=(1,),
            padding=((CONV_W // 2, CONV_W // 2),), dimension_numbers=('NWC', 'WIO', 'NWC'),
            feature_group_count=u.shape[-1]) + conv_b[l].astype(u.dtype)
        gate, up = jnp.split(u, 2, axis=-1)
        x = x + (jax.nn.silu(gate) * up) @ w_down[l]
    return x
```

```python
import numpy as np
from contextlib import ExitStack
import concourse.bass as bass
import concourse.mybir as mybir
from concourse.bass_utils import run_bass_kernel_spmd

F32 = mybir.dt.float32
BF16 = mybir.dt.bfloat16
ALU = mybir.AluOpType
AF = mybir.ActivationFunctionType

D = 4096
KC = 32
NBLK = 25
TOK = NBLK * 128
NQB = 17
NQ = NQB * 128
DFF = 11008
FC = 86
EPS = 1e-6
NEG = -30000.0
NA_T = 14
DL_T = 20


class SemC:
    def __init__(self, nc, es, name, unit):
        self.h = es.enter_context(nc.semaphore(name))
        self.unit = unit
        self.count = 0


class Eng:
    def __init__(self, nc, es, e, name):
        self.e = e
        self.name = name
        self.sc = SemC(nc, es, "s_" + name, 1)
        self.seen = {}
        self.pr = []
        self.pw = []


class Buf:
    __slots__ = ("writer", "readers", "pend")

    def __init__(self):
        self.writer = None
        self.readers = []
        self.pend = None


def op(E, fn, reads=(), writes=(), sig=True, dma=None, selfdep=False):
    deps = {}

    def add(tok):
        if tok is None:
            return
        sc, v = tok
        if deps.get(sc, 0) < v:
            deps[sc] = v

    for b in reads:
        assert b.pend is None or b.pend is E, "pending access by other engine"
        add(b.writer)
    for b in writes:
        assert b.pend is None or b.pend is E, "pending access by other engine"
        add(b.writer)
        for t in b.readers:
            add(t)
    for sc, v in deps.items():
        if sc is E.sc and dma is None and not selfdep:
            continue
        if E.seen.get(sc, 0) < v:
            E.e.wait_ge(sc.h, v * sc.unit)
            E.seen[sc] = v
    ins = fn()
    if dma is not None:
        dma.count += 1
        ins.then_inc(dma.h, 16)
        tok = (dma, dma.count)
        for b in reads:
            b.readers.append(tok)
        for b in writes:
            b.writer = tok
            b.readers = []
        return
    E.pr.extend(reads)
    E.pw.extend(writes)
    for b in reads:
        b.pend = E
    for b in writes:
        b.pend = E
    if sig:
        E.sc.count += 1
        ins.then_inc(E.sc.h, 1)
        tok = (E.sc, E.sc.count)
        for b in E.pr:
            b.readers.append(tok)
            b.pend = None
        for b in E.pw:
            b.writer = tok
            b.readers = []
            b.pend = None
        E.pr = []
        E.pw = []


def build(debug_outs=(), upto=9):
    nc = bass.Bass("TRN2", target_bir_lowering=False)

    def dram_in(name, shape, dt=F32):
        return nc.dram_tensor(name, list(shape), dt, kind="ExternalInput").ap()

    def dram_tmp(name, shape, dt):
        kind = "ExternalOutput" if name in debug_outs else "Internal"
        return nc.dram_tensor(name, list(shape), dt, kind=kind).ap()

    xw = dram_in("xw", [TOK, D])
    w_in = dram_in("w_in", [D, 3 * D])
    w_out = dram_in("w_out", [D, D])
    w_up = dram_in("w_up", [D, 2 * DFF])
    w_down = dram_in("w_down", [DFF, D])
    g1T_d = dram_in("g1T", [128, KC])
    g2T_d = dram_in("g2T", [128, KC])
    goT_d = dram_in("goT", [128, KC])
    qkg_d = dram_in("qkg", [128, 4])
    convp_d = dram_in("convp", [128, 2 * FC, 4])
    nab_d = dram_in("nab", [16, 128, NA_T, 512])
    dlb_d = dram_in("dlb", [16, 128, DL_T, 512])
    ident_d = dram_in("ident", [128, 128])
    y = nc.dram_tensor("y", [2048, D], F32, kind="ExternalOutput").ap()

    QT_d = dram_tmp("QT_d", [32, 128, NQ], BF16)
    KT_d = dram_tmp("KT_d", [32, 128, TOK], BF16)
    V_d = dram_tmp("V_d", [TOK, D], BF16)
    mixT_d = dram_tmp("mixT_d", [D, NQ], BF16)
    x1_d = dram_tmp("x1_d", [NQ, D], F32)
    actT_d = dram_tmp("actT_d", [DFF, 2048], BF16)

    w_in_v = w_in.rearrange("(c p) n -> p c n", p=128)
    w_out_v = w_out.rearrange("(c p) n -> p c n", p=128)
    w_up_v = w_up.rearrange("(c p) n -> p c n", p=128)
    w_down_v = w_down.rearrange("(c p) n -> p c n", p=128)
    mixT_v = mixT_d.rearrange("(c p) t -> p c t", p=128)
    actT_v = actT_d.rearrange("(c p) t -> p c t", p=128)

    with ExitStack() as ges:
        G = ges.enter_context
        PE = Eng(nc, ges, nc.tensor, "pe")
        ACT = Eng(nc, ges, nc.scalar, "act")
        DVE = Eng(nc, ges, nc.vector, "dve")
        POOL = Eng(nc, ges, nc.gpsimd, "pool")
        SP = Eng(nc, ges, nc.sync, "sp")
        engines = [PE, ACT, DVE, POOL, SP]
        bar = G(nc.semaphore("bar"))
        all_sems = [e.sc for e in engines]
        nbar = [0]
        semctr = [0]

        def dsem(es):
            semctr[0] += 1
            s = SemC(nc, ges, "d%d" % semctr[0], 16)
            all_sems.append(s)
            return s

        def barrier():
            for sc in all_sems:
                if sc.count > 0 and SP.seen.get(sc, 0) < sc.count and sc is not SP.sc:
                    SP.e.wait_ge(sc.h, sc.count * sc.unit)
                    SP.seen[sc] = sc.count
            nbar[0] += 1
            SP.e.sem_inc(bar, 1)
            for e in engines:
                if e is not SP:
                    e.e.wait_ge(bar, nbar[0])
                assert not e.pr and not e.pw
                for sc in all_sems:
                    e.seen[sc] = sc.count

        PB = [G(nc.psum_tensor("pb%d" % i, [128, 512], F32)) for i in range(8)]
        PBb = [Buf() for _ in range(8)]
        ident = G(nc.sbuf_tensor("ident_sb", [128, 128], F32))
        ones = G(nc.sbuf_tensor("ones", [128, 128], BF16))
        epsT = G(nc.sbuf_tensor("epsT", [128, 1], F32))
        zeros = G(nc.sbuf_tensor("zeros", [128, 128], BF16))
        g1T = G(nc.sbuf_tensor("g1T_sb", [128, KC], F32))
        g2T = G(nc.sbuf_tensor("g2T_sb", [128, KC], F32))
        goT = G(nc.sbuf_tensor("goT_sb", [128, KC], F32))
        qkg = G(nc.sbuf_tensor("qkg_sb", [128, 4], F32))
        rstdmix = G(nc.sbuf_tensor("rstdmix", [128, 2 * NQB], F32))
        nst = G(nc.sbuf_tensor("nst", [128, 8, 4], F32))
        cB = Buf()
        nstB = [Buf() for _ in range(8)]
        rstdmixB = Buf()
        cs_ = dsem(ges)
        for dst, src in ((ident, ident_d), (g1T, g1T_d), (g2T, g2T_d), (goT, goT_d), (qkg, qkg_d)):
            op(SP, lambda dst=dst, src=src: SP.e.dma_start(out=dst[:], in_=src[:, :]), writes=[cB], dma=cs_)
        op(DVE, lambda: DVE.e.memset(ones[:], 1.0), writes=[cB], sig=False)
        op(DVE, lambda: DVE.e.memset(epsT[:], EPS), writes=[cB], sig=False)
        op(DVE, lambda: DVE.e.memset(zeros[:], 0.0), writes=[cB], sig=False)
        op(DVE, lambda: DVE.e.tensor_scalar(out=qkg[:, 0:1], in0=qkg[:, 0:1], scalar1=float(128 ** -0.5),
                                            scalar2=None, op0=ALU.mult), reads=[cB], writes=[cB], sig=False)
        op(DVE, lambda: DVE.e.tensor_scalar(out=qkg[:, 2:3], in0=qkg[:, 2:3], scalar1=float(128 ** -0.5),
                                            scalar2=None, op0=ALU.mult), reads=[cB], writes=[cB], sig=True)
        barrier()

        def norm_transpose(st, src_rows, np_, gT, dst, dstB, col0, slot):
            xt, xtB, junk, junkB, xs = st["xt"], st["xtB"], st["junk"], st["junkB"], st["xs"]
            s = slot % len(xt)
            nb = nstB[s]
            op(SP, lambda: SP.e.dma_start(out=xt[s][0:np_, :], in_=src_rows), writes=[xtB[s]], dma=xs[s])
            op(ACT, lambda: ACT.e.memzero(nst[0:np_, s, 0:1]), writes=[nb])
            op(ACT, lambda: ACT.e.activation(out=junk[0:np_, :], in_=xt[s][0:np_, :], func=AF.Square,
                                             accum_out=nst[0:np_, s, 0:1]),
               reads=[xtB[s]], writes=[junkB, nb], selfdep=True)
            op(ACT, lambda: ACT.e.activation(out=nst[0:np_, s, 1:2], in_=nst[0:np_, s, 0:1], func=AF.Ln,
                                             scale=1.0 / D, bias=epsT[0:np_, 0:1]),
               reads=[nb, cB], writes=[nb], selfdep=True)
            op(ACT, lambda: ACT.e.activation(out=nst[0:np_, s, 3:4], in_=nst[0:np_, s, 1:2], func=AF.Exp, scale=-0.5),
               reads=[nb], writes=[nb], selfdep=True)
            op(ACT, lambda: ACT.e.activation(out=xt[s][0:np_, :], in_=xt[s][0:np_, :], func=AF.Copy,
                                             scale=nst[0:np_, s, 3:4]), reads=[nb, xtB[s]], writes=[xtB[s]],
               selfdep=True)
            for q in range(8):
                bk = st["tb"][q % 2]
                for j in range(4):
                    kc = q * 4 + j
                    op(PE, lambda kc=kc, j=j, bk=bk: PE.e.transpose(
                        out=PB[bk][:, j * 128:j * 128 + np_], in_=xt[s][0:np_, kc * 128:(kc + 1) * 128],
                        identity=ident[0:np_, 0:np_]),
                       reads=[xtB[s], cB], writes=[PBb[bk]], sig=(j == 3))
                for j in range(4):
                    kc = q * 4 + j
                    op(DVE, lambda kc=kc, j=j, bk=bk: DVE.e.tensor_scalar(
                        out=dst[:, kc, col0:col0 + np_], in0=PB[bk][:, j * 128:j * 128 + np_],
                        scalar1=gT[:, kc:kc + 1], scalar2=None, op0=ALU.mult),
                       reads=[PBb[bk], cB], writes=[dstB], sig=(j == 3))

        def norm_state(es, tb, nslots=2, own_junk=True):
            st = {}
            semctr[0] += 1
            uid = semctr[0]
            st["xt"] = [es.enter_context(nc.sbuf_tensor("xt%d_%d" % (i, uid), [128, D], F32)) for i in range(nslots)]
            st["xtB"] = [Buf() for _ in range(nslots)]
            if own_junk:
                st["junk"] = es.enter_context(nc.sbuf_tensor("junk%d" % uid, [128, D], BF16))
                st["junkB"] = Buf()
            st["xs"] = [dsem(es) for _ in range(nslots)]
            st["tb"] = tb
            return st

        if upto < 2:
            return nc
        SGS = [(0, 8), (8, 17), (17, 25)]
        need_hi = {0: 17, 1: 19, 2: 19, 3: 17, 4: 25, 5: 25}
        with ExitStack() as es:
            E_ = es.enter_context
            st = norm_state(es, (6, 7), nslots=3, own_junk=False)
            hT = E_(nc.sbuf_tensor("hT", [128, KC, 9 * 128], BF16))
            hTB = [Buf() for _ in range(9)]
            wsl = [E_(nc.sbuf_tensor("wsl%d" % i, [128, KC, 512], BF16)) for i in range(2)]
            wslB = [Buf(), Buf()]
            wsem = [dsem(es), dsem(es)]
            sq = [E_(nc.sbuf_tensor("sq%d" % i, [128, 512], BF16)) for i in range(2)]
            sqB = [Buf(), Buf()]
            rt = [E_(nc.sbuf_tensor("rt%d" % i, [128, 512], F32)) for i in range(2)]
            rtB = [Buf(), Buf()]
            ri = [E_(nc.sbuf_tensor("ri%d" % i, [128, 512], F32)) for i in range(2)]
            riB = [Buf(), Buf()]
            og = [E_(nc.sbuf_tensor("og%d" % i, [128, 512], BF16)) for i in range(3)]
            ogB = [Buf() for _ in range(3)]
            osem = [dsem(es) for _ in range(3)]
            ABK = (0, 1, 2)
            BBK = (3, 4)
            cnt = {"a": 0, "b": 0, "o": 0, "w": 0, "u": 0, "x": 0}
            pending_ones = []

            def flush_ones():
                while pending_ones:
                    pending_ones.pop(0)()

            for (b0, b1) in SGS:
                jslot = (cnt["w"] + 1) % 2
                st["junk"] = wsl[jslot][:, 0:8, :].rearrange("p a b -> p (a b)")
                st["junkB"] = wslB[jslot]
                for b in range(b0, b1):
                    norm_transpose(st, xw[b * 128:(b + 1) * 128, :], 128, g1T, hT, hTB[b - b0], (b - b0) * 128,
                                   cnt["x"])
                    cnt["x"] += 1
                slab_order = list(range(24))
                if b0 >= 17:
                    slab_order = [x for pair in zip(range(16, 24), range(4, 12)) for x in pair] + [0, 1, 2, 3, 12, 13, 14, 15]
                for slab in slab_order:
                    typ = slab // 4
                    hi = min(b1, need_hi[typ])
                    if hi <= b0:
                        continue
                    ws = cnt["w"] % 2
                    cnt["w"] += 1
                    op(POOL, lambda ws=ws, slab=slab: POOL.e.dma_start(
                        out=wsl[ws][:], in_=w_in_v[:, :, slab * 512:(slab + 1) * 512]),
                       writes=[wslB[ws]], dma=wsem[ws])
                    if typ in (2, 5):
                        vc0 = (slab - 8) * 512 if typ == 2 else 2048 + (slab - 20) * 512
                        for b in range(b0, hi):
                            a = ABK[cnt["a"] % 3]
                            cnt["a"] += 1
                            lb = b - b0
                            for kc in range(KC):
                                op(PE, lambda kc=kc, a=a, lb=lb, ws=ws: PE.e.matmul(
                                    PB[a][:, :], lhsT=hT[:, kc, lb * 128:(lb + 1) * 128], rhs=wsl[ws][:, kc, :],
                                    start=(kc == 0), stop=(kc == KC - 1)),
                                   reads=[hTB[lb], wslB[ws]], writes=[PBb[a]], sig=(kc == KC - 1))
                            flush_ones()
                            o = cnt["o"] % 3
                            cnt["o"] += 1
                            op(ACT, lambda a=a, o=o: ACT.e.activation(out=og[o][:, :], in_=PB[a][:, :], func=AF.Copy),
                               reads=[PBb[a]], writes=[ogB[o]])
                            op(SP, lambda o=o, b=b, vc0=vc0: SP.e.dma_start(
                                out=V_d[b * 128:(b + 1) * 128, vc0:vc0 + 512], in_=og[o][:, :]),
                               reads=[ogB[o]], dma=osem[o])
                    else:
                        isq = typ in (0, 3)
                        gcol = {0: 0, 1: 1, 3: 2, 4: 3}[typ]
                        hbase = {0: 0, 1: 0, 3: 16, 4: 16}[typ] + (slab % 4) * 4
                        dstT = QT_d if isq else KT_d
                        groups = []
                        bb = b0
                        while bb < hi:
                            e2 = min(hi, bb + 4)
                            groups.append((bb, e2))
                            bb = e2
                        for hh in range(4):
                            head = hbase + hh
                            for (gb0, gb1) in groups:
                                n = (gb1 - gb0) * 128
                                c0 = (gb0 - b0) * 128
                                a = ABK[cnt["a"] % 3]
                                cnt["a"] += 1
                                bk = BBK[cnt["b"] % 2]
                                u = cnt["u"] % 2
                                cnt["b"] += 1
                                cnt["u"] += 1
                                rb = [hTB[i - b0] for i in range(gb0, gb1)]
                                for kc in range(KC):
                                    op(PE, lambda kc=kc, a=a, ws=ws, hh=hh, c0=c0, n=n: PE.e.matmul(
                                        PB[a][:, 0:n], lhsT=wsl[ws][:, kc, hh * 128:(hh + 1) * 128],
                                        rhs=hT[:, kc, c0:c0 + n], start=(kc == 0), stop=(kc == KC - 1)),
                                       reads=rb + [wslB[ws]], writes=[PBb[a]], sig=(kc == KC - 1))
                                flush_ones()
                                op(ACT, lambda a=a, u=u, n=n: ACT.e.activation(
                                    out=sq[u][:, 0:n], in_=PB[a][:, 0:n], func=AF.Square),
                                   reads=[PBb[a]], writes=[sqB[u]])

                                def ones_mm(a=a, bk=bk, u=u, n=n, head=head, gb0=gb0, gcol=gcol, dstT=dstT):
                                    op(PE, lambda: PE.e.matmul(PB[bk][:, 0:n], lhsT=ones[:, :], rhs=sq[u][:, 0:n],
                                                               start=True, stop=True),
                                       reads=[sqB[u], cB], writes=[PBb[bk]])
                                    op(ACT, lambda: ACT.e.activation(out=rt[u][:, 0:n], in_=PB[bk][:, 0:n],
                                                                     func=AF.Sqrt, scale=1.0 / 128, bias=epsT[:, 0:1]),
                                       reads=[PBb[bk], cB], writes=[rtB[u]])
                                    op(DVE, lambda: DVE.e.reciprocal(out=ri[u][:, 0:n], in_=rt[u][:, 0:n]),
                                       reads=[rtB[u]], writes=[riB[u]], sig=False)
                                    o = cnt["o"] % 3
                                    cnt["o"] += 1
                                    op(DVE, lambda: DVE.e.scalar_tensor_tensor(
                                        out=og[o][:, 0:n], in0=PB[a][:, 0:n], scalar=qkg[:, gcol:gcol + 1],
                                        in1=ri[u][:, 0:n], op0=ALU.mult, op1=ALU.mult),
                                       reads=[PBb[a], riB[u], cB], writes=[ogB[o]])
                                    op(SP, lambda: SP.e.dma_start(out=dstT[head, :, gb0 * 128:gb0 * 128 + n],
                                                                  in_=og[o][:, 0:n]),
                                       reads=[ogB[o]], dma=osem[o])

                                pending_ones.append(ones_mm)
                flush_ones()
            barrier()

        if upto < 3:
            return nc
        with ExitStack() as es:
            E_ = es.enter_context
            qt = [E_(nc.sbuf_tensor("qt%d" % i, [128, NQ], BF16)) for i in range(2)]
            kt = [E_(nc.sbuf_tensor("kt%d" % i, [128, TOK], BF16)) for i in range(2)]
            bt = [E_(nc.sbuf_tensor("bt%d" % i, [128, DL_T, 512], F32)) for i in range(2)]
            vq = [E_(nc.sbuf_tensor("vq%d" % i, [128, NBLK, 512], BF16)) for i in range(2)]
            qB = [Buf(), Buf()]
            kB = [Buf(), Buf()]
            bB = [Buf(), Buf()]
            vqB = [Buf(), Buf()]
            qsem = [dsem(es), dsem(es)]
            ksem = [dsem(es), dsem(es)]
            bsem = [dsem(es), dsem(es)]
            vsem = [dsem(es), dsem(es)]
            NSL = 8
            tt = [E_(nc.sbuf_tensor("tt%d" % i, [128, 512], F32)) for i in range(NSL)]
            ttB = [Buf() for _ in range(NSL)]
            pp = [E_(nc.sbuf_tensor("pp%d" % i, [128, 512], BF16)) for i in range(NSL)]
            ppB = [Buf() for _ in range(NSL)]
            rl = [E_(nc.sbuf_tensor("rl%d" % i, [128, 512], F32)) for i in range(2)]
            on = [E_(nc.sbuf_tensor("on%d" % i, [128, 512], F32)) for i in range(2)]
            sq3 = [E_(nc.sbuf_tensor("sqa%d" % i, [128, 512], BF16)) for i in range(2)]
            mx = [E_(nc.sbuf_tensor("mx%d" % i, [128, 512], BF16)) for i in range(4)]
            rlB = [Buf(), Buf()]
            onB = [Buf(), Buf()]
            sq3B = [Buf(), Buf()]
            mxB = [Buf() for _ in range(4)]
            msem = [dsem(es) for _ in range(4)]
            SBK = (0, 1, 2)
            OBK = (3, 5)
            LBK = (4, 6)
            SSB = 7
            LOOK = 3
            op(DVE, lambda: DVE.e.memset(PB[SSB][:, :], 0.0), writes=[PBb[SSB]])
            na_rng, dl_rng = _col_ranges()
            ci = 0
            gi = 0
            fin_prev = None
            epi_prev = None
            def load_vq(hq):
                vs = hq % 2
                nvb = 19 if hq < 4 else NBLK
                op(SP, lambda: SP.e.dma_start(
                    out=vq[vs][:, 0:nvb, :],
                    in_=V_d[0:nvb * 128, hq * 512:(hq + 1) * 512].rearrange("(k p) c -> p k c", p=128)),
                   writes=[vqB[vs]], dma=vsem[vs])

            def load_head(h):
                na = h < 16
                hs = h % 2
                nkb = 19 if na else NBLK
                ntile = NA_T if na else DL_T
                bsrc = nab_d[h] if na else dlb_d[h - 16]
                op(SP, lambda: SP.e.dma_start(out=qt[hs][:, :], in_=QT_d[h, :, :]),
                   writes=[qB[hs]], dma=qsem[hs])
                op(SP, lambda: SP.e.dma_start(out=kt[hs][:, 0:nkb * 128], in_=KT_d[h, :, 0:nkb * 128]),
                   writes=[kB[hs]], dma=ksem[hs])
                op(SP, lambda: SP.e.dma_start(out=bt[hs][:, 0:ntile, :], in_=bsrc),
                   writes=[bB[hs]], dma=bsem[hs])

            mxc = [0]

            def _epiA(gs, nq, ob, lbk, h, q0):
                op(ACT, lambda: ACT.e.activation(out=rl[gs][:, 0:nq], in_=PB[lbk][:, 0:nq], func=AF.Ln),
                   reads=[PBb[lbk]], writes=[rlB[gs]], sig=False)
                op(ACT, lambda: ACT.e.activation(out=rl[gs][:, 0:nq], in_=rl[gs][:, 0:nq], func=AF.Exp, scale=-1.0),
                   reads=[rlB[gs]], writes=[rlB[gs]])

            def _epiB(gs, nq, ob, lbk, h, q0):
                op(DVE, lambda: DVE.e.tensor_tensor(out=on[gs][:, 0:nq], in0=PB[ob][:, 0:nq],
                                                    in1=rl[gs][:, 0:nq], op=ALU.mult),
                   reads=[PBb[ob], rlB[gs]], writes=[onB[gs]])
                op(DVE, lambda: DVE.e.tensor_tensor(out=sq3[gs][:, 0:nq], in0=on[gs][:, 0:nq],
                                                    in1=on[gs][:, 0:nq], op=ALU.mult),
                   reads=[onB[gs]], writes=[sq3B[gs]])

            def _epiC(gs, nq, ob, lbk, h, q0):
                ms_ = mxc[0] % 4
                mxc[0] += 1
                op(ACT, lambda: ACT.e.activation(out=mx[ms_][:, 0:nq], in_=on[gs][:, 0:nq], func=AF.Copy,
                                                 scale=goT[:, h:h + 1]),
                   reads=[onB[gs], cB], writes=[mxB[ms_]])
                op(SP, lambda: SP.e.dma_start(out=mixT_d[h * 128:(h + 1) * 128, q0:q0 + nq], in_=mx[ms_][:, 0:nq]),
                   reads=[mxB[ms_]], dma=msem[ms_])

            load_vq(0)
            load_head(0)
            for hq in range(8):
                vs = hq % 2
                for hh in range(4):
                    h = hq * 4 + hh
                    na = h < 16
                    hs = h % 2
                    if h + 1 < 32:
                        load_head(h + 1)
                    if hh == 0 and hq + 1 < 8:
                        load_vq(hq + 1)
                    first = (h % 16 == 0)
                    last = (h % 16 == 15)
                    typ = 0 if na else 1
                    for g in range(5):
                        nq = 512 if g < 4 else 128
                        q0 = g * 512
                        if na:
                            if g == 0:
                                kbs = [(kb, 8 + kb) for kb in range(0, 6)]
                            elif g < 4:
                                kbs = [(kb, kb - 4 * g + 2) for kb in range(4 * g - 2, 4 * g + 6)]
                            else:
                                kbs = [(kb, kb - 16 + 2) for kb in range(14, 19)]
                        else:
                            if g < 4:
                                kbs = [(kb, kb - 4 * g + 8) for kb in range(max(0, 4 * g - 8), min(24, 4 * g + 11) + 1)]
                            else:
                                kbs = [(kb, kb - 16 + 8) for kb in range(8, 25)]
                        ob = OBK[gi % 2]
                        lbk = LBK[gi % 2]
                        gs = gi % 2
                        gi += 1
                        nk = len(kbs)
                        slots = []

                        crng = na_rng if na else dl_rng
                        full = [i for i, (_, t) in enumerate(kbs) if crng[t][0] == 0 and crng[t][1] >= nq]
                        if full:
                            kbs.insert(0, kbs.pop(full[0]))
                        cr = [(0, nq) if i == 0 else (min(crng[t][0], nq), min(crng[t][1], nq))
                              for i, (_, t) in enumerate(kbs)]
                        assert all(b_ > a_ for a_, b_ in cr)

                        def emit_S(i):
                            nonlocal ci
                            kb, tid = kbs[i]
                            a_, b_ = cr[i]
                            sl = ci % NSL
                            sb = SBK[ci % len(SBK)]
                            ci += 1
                            slots.append(sl)
                            op(PE, lambda: PE.e.matmul(PB[sb][:, a_:b_], lhsT=kt[hs][:, kb * 128:(kb + 1) * 128],
                                                       rhs=qt[hs][:, q0 + a_:q0 + b_], start=True, stop=True),
                               reads=[qB[hs], kB[hs]], writes=[PBb[sb]])
                            op(DVE, lambda: DVE.e.tensor_tensor(out=tt[sl][:, a_:b_], in0=PB[sb][:, a_:b_],
                                                                in1=bt[hs][:, tid, a_:b_], op=ALU.add),
                               reads=[PBb[sb], bB[hs]], writes=[ttB[sl]])
                            op(ACT, lambda: ACT.e.activation(out=pp[sl][:, a_:b_], in_=tt[sl][:, a_:b_], func=AF.Exp),
                               reads=[ttB[sl]], writes=[ppB[sl]])

                        def emit_PV(i):
                            kb, tid = kbs[i]
                            a_, b_ = cr[i]
                            sl = slots[i]
                            op(PE, lambda: PE.e.matmul(PB[ob][:, a_:b_], lhsT=vq[vs][:, kb, hh * 128:(hh + 1) * 128],
                                                       rhs=pp[sl][:, a_:b_], start=(i == 0), stop=(i == nk - 1),
                                                       skip_group_check=True),
                               reads=[vqB[vs], ppB[sl]], writes=[PBb[ob]], sig=False)
                            op(PE, lambda: PE.e.matmul(PB[lbk][:, a_:b_], lhsT=ones[:, :], rhs=pp[sl][:, a_:b_],
                                                       start=(i == 0), stop=(i == nk - 1), skip_group_check=True),
                               reads=[cB, ppB[sl]], writes=[PBb[lbk]], sig=True)

                        for i in range(min(LOOK, nk)):
                            emit_S(i)
                        for i in range(nk):
                            if i + LOOK < nk:
                                emit_S(i + LOOK)
                            emit_PV(i)
                            if epi_prev is not None:
                                if i == 0:
                                    _epiA(*epi_prev)
                                if i == min(2, nk - 1):
                                    _epiB(*epi_prev)
                                if i == min(4, nk - 1):
                                    _epiC(*epi_prev)
                                    fin_prev()
                                    fin_prev = None
                                    epi_prev = None

                        epi_next = (gs, nq, ob, lbk, h, q0)
                        def fin(gs=gs, nq=nq, g=g, typ=typ, first=first, last=last):
                            nt = nq // 128
                            for j in range(nt):
                                col = typ * NQB + g * 4 + j
                                op(PE, lambda: PE.e.matmul(PB[SSB][:, col:col + 1], lhsT=sq3[gs][:, j * 128:(j + 1) * 128],
                                                           rhs=ones[:, 0:1], start=False, stop=False,
                                                           skip_group_check=True),
                                   reads=[sq3B[gs], cB], writes=[PBb[SSB]], sig=(j == nt - 1))

                        fin_prev = fin
                        epi_prev = epi_next
            _epiA(*epi_prev)
            _epiB(*epi_prev)
            _epiC(*epi_prev)
            fin_prev()
            op(DVE, lambda: DVE.e.tensor_scalar(out=rstdmix[:, :], in0=PB[SSB][:, 0:2 * NQB], scalar1=1.0 / 2048,
                                                scalar2=EPS, op0=ALU.mult, op1=ALU.add),
               reads=[PBb[SSB]], writes=[rstdmixB])
            op(ACT, lambda: ACT.e.activation(out=rstdmix[:, :], in_=rstdmix[:, :], func=AF.Sqrt),
               reads=[rstdmixB], writes=[rstdmixB])
            op(DVE, lambda: DVE.e.reciprocal(out=rstdmix[:, :], in_=rstdmix[:, :]), reads=[rstdmixB], writes=[rstdmixB])
            barrier()

        if upto < 4:
            return nc
        with ExitStack() as es:
            E_ = es.enter_context
            mT = E_(nc.sbuf_tensor("mT", [128, KC, 9 * 128], BF16))
            mTB = [Buf() for _ in range(3)]
            mTs = [dsem(es) for _ in range(3)]
            wsl = [E_(nc.sbuf_tensor("wo%d" % i, [128, KC, 512], BF16)) for i in range(2)]
            wslB = [Buf(), Buf()]
            wsem = [dsem(es), dsem(es)]
            xs_ = [E_(nc.sbuf_tensor("xs%d" % i, [128, 512], F32)) for i in range(3)]
            xsB = [Buf() for _ in range(3)]
            xsem = [dsem(es) for _ in range(3)]
            o1 = [E_(nc.sbuf_tensor("o1%d" % i, [128, 512], F32)) for i in range(3)]
            o1B = [Buf() for _ in range(3)]
            o2 = [E_(nc.sbuf_tensor("o2%d" % i, [128, 512], F32)) for i in range(3)]
            o2B = [Buf() for _ in range(3)]
            o2sem = [dsem(es) for _ in range(3)]
            u = 0
            w = 0
            for (b0, b1) in ((0, 9), (9, 17)):
                nt = (b1 - b0) * 128
                for pi in range(3):
                    c0 = pi * 384
                    c1 = min(nt, c0 + 384)
                    op(SP, lambda b0=b0, c0=c0, c1=c1: SP.e.dma_start(
                        out=mT[:, :, c0:c1], in_=mixT_v[:, :, b0 * 128 + c0:b0 * 128 + c1]),
                       writes=[mTB[pi]], dma=mTs[pi])
                for cs in range(8):
                    ws = w % 2
                    w += 1
                    op(POOL, lambda ws=ws, cs=cs: POOL.e.dma_start(out=wsl[ws][:], in_=w_out_v[:, :, cs * 512:(cs + 1) * 512]),
                       writes=[wslB[ws]], dma=wsem[ws])
                    for b in range(b0, b1):
                        lb = b - b0
                        k3 = u % 3
                        pa = (0, 2, 4)[k3]
                        pb_ = (1, 3, 5)[k3]
                        u += 1
                        op(SP, lambda k3=k3, b=b, cs=cs: SP.e.dma_start(
                            out=xs_[k3][:, :], in_=xw[b * 128:(b + 1) * 128, cs * 512:(cs + 1) * 512]),
                           writes=[xsB[k3]], dma=xsem[k3])
                        for kc in range(16):
                            op(PE, lambda kc=kc, pa=pa, lb=lb, ws=ws: PE.e.matmul(
                                PB[pa][:, :], lhsT=mT[:, kc, lb * 128:(lb + 1) * 128], rhs=wsl[ws][:, kc, :],
                                start=(kc == 0), stop=(kc == 15)),
                               reads=[mTB[lb // 3], wslB[ws]], writes=[PBb[pa]], sig=(kc == 15))
                        for kc in range(16, 32):
                            op(PE, lambda kc=kc, pb_=pb_, lb=lb, ws=ws: PE.e.matmul(
                                PB[pb_][:, :], lhsT=mT[:, kc, lb * 128:(lb + 1) * 128], rhs=wsl[ws][:, kc, :],
                                start=(kc == 16), stop=(kc == 31)),
                               reads=[mTB[lb // 3], wslB[ws]], writes=[PBb[pb_]], sig=(kc == 31))
                        op(DVE, lambda k3=k3, pa=pa, b=b: DVE.e.scalar_tensor_tensor(
                            out=o1[k3][:, :], in0=PB[pa][:, :], scalar=rstdmix[:, b:b + 1], in1=xs_[k3][:, :],
                            op0=ALU.mult, op1=ALU.add),
                           reads=[PBb[pa], rstdmixB, xsB[k3]], writes=[o1B[k3]], sig=False)
                        op(DVE, lambda k3=k3, pb_=pb_, b=b: DVE.e.scalar_tensor_tensor(
                            out=o2[k3][:, :], in0=PB[pb_][:, :], scalar=rstdmix[:, NQB + b:NQB + b + 1], in1=o1[k3][:, :],
                            op0=ALU.mult, op1=ALU.add),
                           reads=[PBb[pb_], rstdmixB, o1B[k3]], writes=[o2B[k3]])
                        op(SP, lambda k3=k3, b=b, cs=cs: SP.e.dma_start(
                            out=x1_d[b * 128:(b + 1) * 128, cs * 512:(cs + 1) * 512], in_=o2[k3][:, :]),
                           reads=[o2B[k3]], dma=o2sem[k3])
            barrier()

        if upto < 5:
            return nc
        with ExitStack() as es:
            E_ = es.enter_context
            st = norm_state(es, (6, 7), nslots=2, own_junk=False)
            NCOL = 1026
            GW = 342
            h2 = E_(nc.sbuf_tensor("h2", [128, KC, NCOL], BF16))
            h2B = [Buf() for _ in range(10)]
            cvp = E_(nc.sbuf_tensor("cvp", [128, 2 * FC, 4], F32))
            cvs = dsem(es)
            op(SP, lambda: SP.e.dma_start(out=cvp[:], in_=convp_d[:, :, :]), writes=[cB], dma=cvs)
            wg = [E_(nc.sbuf_tensor("wg%d" % i, [128, KC, 256], BF16)) for i in range(2)]
            wu = [E_(nc.sbuf_tensor("wu%d" % i, [128, KC, 256], BF16)) for i in range(2)]
            wgB = [Buf(), Buf()]
            wuB = [Buf(), Buf()]
            wgs = [dsem(es), dsem(es)]
            wus = [dsem(es), dsem(es)]
            ub = [[E_(nc.sbuf_tensor("ub%d_%d" % (t, i), [128, NCOL], F32)) for i in range(2)] for t in range(2)]
            ubB = [[Buf(), Buf()], [Buf(), Buf()]]
            cgt = [E_(nc.sbuf_tensor("cg%d" % i, [128, 1024], F32)) for i in range(2)]
            cut = [E_(nc.sbuf_tensor("cu%d" % i, [128, 1024], F32)) for i in range(2)]
            cgB = [Buf(), Buf()]
            cuB = [Buf(), Buf()]
            ab = [E_(nc.sbuf_tensor("ab%d" % i, [128, 1024], BF16)) for i in range(2)]
            abB = [Buf(), Buf()]
            absem = [dsem(es), dsem(es)]
            xcnt = 0
            wcnt = 0
            fcnt = 0
            ucnt = 0
            for sg in range(2):
                t0 = sg * 1024
                jslot = (wcnt + 1) % 2
                st["junk"] = wu[jslot][:, 0:16, :].rearrange("p a b -> p (a b)")
                st["junkB"] = wuB[jslot]
                if sg == 0:
                    op(DVE, lambda: DVE.e.memset(h2[:, :, 0:1], 0.0), writes=[h2B[0]])
                else:
                    norm_transpose(st, x1_d[t0 - 1:t0, :], 1, g2T, h2, h2B[0], 0, xcnt)
                    xcnt += 1
                for i in range(8):
                    b = sg * 8 + i
                    norm_transpose(st, x1_d[b * 128:(b + 1) * 128, :], 128, g2T, h2, h2B[1 + i], 1 + i * 128, xcnt)
                    xcnt += 1
                norm_transpose(st, x1_d[t0 + 1024:t0 + 1025, :], 1, g2T, h2, h2B[9], 1025, xcnt)
                xcnt += 1
                for fq in range(43):
                    ws = wcnt % 2
                    wcnt += 1
                    op(POOL, lambda ws=ws, fq=fq: POOL.e.dma_start(out=wg[ws][:], in_=w_up_v[:, :, fq * 256:(fq + 1) * 256]),
                       writes=[wgB[ws]], dma=wgs[ws])
                    op(POOL, lambda ws=ws, fq=fq: POOL.e.dma_start(
                        out=wu[ws][:], in_=w_up_v[:, :, DFF + fq * 256:DFF + (fq + 1) * 256]),
                       writes=[wuB[ws]], dma=wus[ws])
                    for fc in range(2):
                        f = fq * 2 + fc
                        fs = fcnt % 2
                        fcnt += 1
                        for typ in range(2):
                            wt, wtB = (wg, wgB) if typ == 0 else (wu, wuB)
                            bks = (0, 1, 2) if ucnt % 2 == 0 else (3, 4, 5)
                            ucnt += 1
                            for kc in range(KC):
                                for gr in range(3):
                                    op(PE, lambda kc=kc, gr=gr, wt=wt, ws=ws, fc=fc, bks=bks: PE.e.matmul(
                                        PB[bks[gr]][:, 0:GW], lhsT=wt[ws][:, kc, fc * 128:(fc + 1) * 128],
                                        rhs=h2[:, kc, gr * GW:(gr + 1) * GW], start=(kc == 0), stop=(kc == KC - 1)),
                                       reads=h2B + [wtB[ws]], writes=[PBb[bks[gr]]],
                                       sig=(kc == KC - 1 and gr == 2))
                            for gr in range(3):
                                op(ACT, lambda gr=gr, typ=typ, fs=fs, bks=bks: ACT.e.activation(
                                    out=ub[typ][fs][:, gr * GW:(gr + 1) * GW], in_=PB[bks[gr]][:, 0:GW], func=AF.Copy),
                                   reads=[PBb[bks[gr]]], writes=[ubB[typ][fs]], sig=(gr == 2))
                        for typ in range(2):
                            ch = f if typ == 0 else FC + f
                            ct, ctB = (cgt, cgB) if typ == 0 else (cut, cuB)
                            uu = ub[typ][fs]
                            uB = ubB[typ][fs]
                            op(DVE, lambda ct=ct, uu=uu, ch=ch, fs=fs: DVE.e.tensor_scalar(
                                out=ct[fs][:, :], in0=uu[:, 1:1025], scalar1=cvp[:, ch, 1:2], scalar2=cvp[:, ch, 3:4],
                                op0=ALU.mult, op1=ALU.add), reads=[uB, cB], writes=[ctB[fs]], sig=False)
                            op(DVE, lambda ct=ct, uu=uu, ch=ch, fs=fs: DVE.e.scalar_tensor_tensor(
                                out=ct[fs][:, :], in0=uu[:, 0:1024], scalar=cvp[:, ch, 0:1], in1=ct[fs][:, :],
                                op0=ALU.mult, op1=ALU.add), reads=[uB, cB, ctB[fs]], writes=[ctB[fs]], sig=False)
                            op(DVE, lambda ct=ct, uu=uu, ch=ch, fs=fs: DVE.e.scalar_tensor_tensor(
                                out=ct[fs][:, :], in0=uu[:, 2:1026], scalar=cvp[:, ch, 2:3], in1=ct[fs][:, :],
                                op0=ALU.mult, op1=ALU.add), reads=[uB, cB, ctB[fs]], writes=[ctB[fs]], sig=True)
                        op(ACT, lambda fs=fs: ACT.e.activation(out=cgt[fs][:, :], in_=cgt[fs][:, :], func=AF.Silu),
                           reads=[cgB[fs]], writes=[cgB[fs]])
                        op(DVE, lambda fs=fs: DVE.e.tensor_tensor(out=ab[fs][:, :], in0=cgt[fs][:, :], in1=cut[fs][:, :],
                                                                  op=ALU.mult),
                           reads=[cgB[fs], cuB[fs]], writes=[abB[fs]])
                        op(SP, lambda fs=fs, f=f, t0=t0: SP.e.dma_start(
                            out=actT_d[f * 128:(f + 1) * 128, t0:t0 + 1024], in_=ab[fs][:, :]),
                           reads=[abB[fs]], dma=absem[fs])
            barrier()

        if upto < 6:
            return nc
        with ExitStack() as es:
            E_ = es.enter_context
            aT = E_(nc.sbuf_tensor("aT", [128, FC, 512], BF16))
            aTB = [Buf() for _ in range(11)]
            aTs = [dsem(es) for _ in range(11)]
            wd = [E_(nc.sbuf_tensor("wd%d" % i, [128, 8, 512], BF16)) for i in range(3)]
            wdB = [Buf() for _ in range(3)]
            wds = [dsem(es) for _ in range(3)]
            x1s = [E_(nc.sbuf_tensor("x1s%d" % i, [128, 512], F32)) for i in range(3)]
            x1B = [Buf() for _ in range(3)]
            x1sem = [dsem(es) for _ in range(3)]
            ys = [E_(nc.sbuf_tensor("ys%d" % i, [128, 512], F32)) for i in range(3)]
            ysB = [Buf() for _ in range(3)]
            ysem = [dsem(es) for _ in range(3)]
            wcnt = 0
            ccnt = 0
            ycnt = 0
            pieces = [(p * 8, min(FC, p * 8 + 8)) for p in range(11)]
            def load_aT(sg):
                t0 = sg * 512
                for pi, (k0, k1) in enumerate(pieces):
                    op(SP, lambda k0=k0, k1=k1: SP.e.dma_start(out=aT[:, k0:k1, :], in_=actT_v[:, k0:k1, t0:t0 + 512]),
                       writes=[aTB[pi]], dma=aTs[pi])

            load_aT(0)
            for sg in range(4):
                t0 = sg * 512
                for cs in range(8):
                    bks = (0, 1, 2, 3) if ccnt % 2 == 0 else (4, 5, 6, 7)
                    ccnt += 1
                    for pi, (k0, k1) in enumerate(pieces):
                        ws = wcnt % 3
                        wcnt += 1
                        op(POOL, lambda ws=ws, k0=k0, k1=k1, cs=cs: POOL.e.dma_start(
                            out=wd[ws][:, 0:k1 - k0, :], in_=w_down_v[:, k0:k1, cs * 512:(cs + 1) * 512]),
                           writes=[wdB[ws]], dma=wds[ws])
                        for t in range(4):
                            for kc in range(k0, k1):
                                op(PE, lambda t=t, kc=kc, ws=ws, k0=k0, bks=bks: PE.e.matmul(
                                    PB[bks[t]][:, :], lhsT=aT[:, kc, t * 128:(t + 1) * 128], rhs=wd[ws][:, kc - k0, :],
                                    start=(kc == 0), stop=(kc == FC - 1)),
                                   reads=[aTB[pi], wdB[ws]], writes=[PBb[bks[t]]],
                                   sig=(kc == k1 - 1 and (t == 3 or kc == FC - 1)))
                    if cs == 7 and sg + 1 < 4:
                        load_aT(sg + 1)
                    for t in range(4):
                        k3 = ycnt % 3
                        ycnt += 1
                        r0 = t0 + t * 128
                        op(SP, lambda k3=k3, r0=r0, cs=cs: SP.e.dma_start(
                            out=x1s[k3][:, :], in_=x1_d[r0:r0 + 128, cs * 512:(cs + 1) * 512]),
                           writes=[x1B[k3]], dma=x1sem[k3])
                        op(DVE, lambda k3=k3, t=t, bks=bks: DVE.e.tensor_tensor(
                            out=ys[k3][:, :], in0=PB[bks[t]][:, :], in1=x1s[k3][:, :], op=ALU.add),
                           reads=[PBb[bks[t]], x1B[k3]], writes=[ysB[k3]])
                        op(SP, lambda k3=k3, r0=r0, cs=cs: SP.e.dma_start(
                            out=y[r0:r0 + 128, cs * 512:(cs + 1) * 512], in_=ys[k3][:, :]),
                           reads=[ysB[k3]], dma=ysem[k3])
            barrier()
    return nc


def _dil_bias():
    kp = np.arange(128)[:, None, None]
    tid = np.arange(DL_T)[None, :, None]
    qi = np.arange(512)[None, None, :]
    d = (tid - 8) * 128 + kp - qi
    ad = np.abs(d)
    mult = (ad <= 64).astype(np.float64) + ((d % 4 == 0) & (ad <= 256)) + ((d % 16 == 0) & (ad <= 1024))
    slopes = np.exp2(-8.0 * np.arange(1, 17, dtype=np.float64) / 16)
    out = np.empty((16, 128, DL_T, 512), np.float32)
    lm = np.log(np.maximum(mult, 1.0))
    for h in range(16):
        out[h] = np.where(mult > 0, -slopes[h] * ad + lm, NEG).astype(np.float32)
    return out


def _na_struct(half):
    kp = np.arange(128)[:, None, None]
    tid = np.arange(NA_T)[None, :, None]
    qi = np.arange(512)[None, None, :]
    qloc = np.where(tid < 8, 512 + qi, qi) + 0 * kp
    kloc = np.where(tid < 8, (2 + tid) * 128 + kp, (tid - 8) * 128 + kp) + 0 * qi
    if half == 0:
        qg, kg = qloc, kloc
    else:
        qg, kg = 4095 - qloc, 4095 - kloc
    qr, qc = qg // 64, qg % 64
    kr, kc = kg // 64, kg % 64
    r0 = np.clip(qr - 4, 0, 56)
    c0 = np.clip(qc - 8, 0, 48)
    ok = (kr >= r0) & (kr < r0 + 8) & (kc >= c0) & (kc < c0 + 16) & (kg >= 0) & (kg < 4096)
    dr = np.clip(kr - qr + 7, 0, 14)
    dc = np.clip(kc - qc + 15, 0, 30)
    return ok, dr, dc


def _na_bias(rel_bias, half):
    ok, dr, dc = _na_struct(half)
    out = np.empty((16, 128, NA_T, 512), np.float32)
    for h in range(16):
        out[h] = np.where(ok, rel_bias[h][dr, dc], np.float32(NEG))
    return out


def _col_ranges():
    def rng(anyok):
        out = []
        for t in range(anyok.shape[0]):
            idx = np.nonzero(anyok[t])[0]
            out.append((int(idx[0]) // 128 * 128, (int(idx[-1]) // 128 + 1) * 128))
        return out
    na_any = (_na_struct(0)[0] | _na_struct(1)[0]).any(axis=0)
    kp = np.arange(128)[:, None, None]
    tid = np.arange(DL_T)[None, :, None]
    qi = np.arange(512)[None, None, :]
    d = (tid - 8) * 128 + kp - qi
    ad = np.abs(d)
    dl_any = ((ad <= 64) | ((d % 4 == 0) & (ad <= 256)) | ((d % 16 == 0) & (ad <= 1024))).any(axis=0)
    return rng(na_any), rng(dl_any)


_NC_CACHE = {}


def kernel(x, norm1_g, w_in, qn_na, kn_na, qn_dil, kn_dil, rel_bias, out_norm_g, w_out, norm2_g,
           w_up, conv_w, conv_b, w_down, _debug_outs=()):
    x = np.asarray(x, np.float32)
    f32c = lambda a: np.ascontiguousarray(np.asarray(a, np.float32))
    tT = lambda g: f32c(np.asarray(g, np.float32).reshape(KC, 128).T)
    shared = {
        "w_in": f32c(w_in[0]), "w_out": f32c(w_out[0]), "w_up": f32c(w_up[0]), "w_down": f32c(w_down[0]),
        "g1T": tT(norm1_g[0]), "g2T": tT(norm2_g[0]), "goT": tT(out_norm_g[0]),
        "qkg": f32c(np.stack([qn_na[0], kn_na[0], qn_dil[0], kn_dil[0]], axis=1)),
        "dlb": _dil_bias(), "ident": np.eye(128, dtype=np.float32),
    }
    rb = np.asarray(rel_bias[0], np.float32)
    nab = [_na_bias(rb, 0), _na_bias(rb, 1)]
    cw = np.asarray(conv_w[0], np.float32)
    cb = np.asarray(conv_b[0], np.float32)
    convp = []
    for half in range(2):
        rows = [cw[0], cw[1], cw[2], cb] if half == 0 else [cw[2], cw[1], cw[0], cb]
        a = np.stack(rows, axis=1).reshape(2 * FC, 128, 4).transpose(1, 0, 2)
        convp.append(f32c(a))
    in_maps = []
    for c in range(8):
        b, half = c // 2, c % 2
        xwin = x[b, 0:TOK] if half == 0 else x[b, 4096 - TOK:4096][::-1]
        m = dict(shared)
        m["xw"] = f32c(xwin)
        m["nab"] = nab[half]
        m["convp"] = convp[half]
        in_maps.append(m)
    key = tuple(_debug_outs)
    if key not in _NC_CACHE:
        _NC_CACHE[key] = build(_debug_outs)
    nc = _NC_CACHE[key]
    res = run_bass_kernel_spmd(nc, in_maps, core_ids=list(range(8)))
    out = np.empty((4, 4096, D), np.float32)
    for c in range(8):
        b, half = c // 2, c % 2
        yl = res.results[c]["y"]
        if half == 0:
            out[b, 0:2048] = yl
        else:
            out[b, 2048:4096] = yl[::-1]
    if _debug_outs:
        return out, res.results
    return out
```

```python
import numpy as np
from contextlib import ExitStack
import concourse.bass as bass
import concourse.mybir as mybir
from concourse.bass_utils import run_bass_kernel_spmd

F32 = mybir.dt.float32
BF16 = mybir.dt.bfloat16
ALU = mybir.AluOpType
AF = mybir.ActivationFunctionType

D = 4096
KC = 32
NBLK = 25
TOK = NBLK * 128
NQB = 17
NQ = NQB * 128
DFF = 11008
FC = 86
EPS = 1e-6
NEG = -30000.0
NA_T = 14
DL_T = 20


class SemC:
    def __init__(self, nc, es, name, unit):
        self.h = es.enter_context(nc.semaphore(name))
        self.unit = unit
        self.count = 0


class Eng:
    def __init__(self, nc, es, e, name):
        self.e = e
        self.name = name
        self.sc = SemC(nc, es, "s_" + name, 1)
        self.seen = {}
        self.pr = []
        self.pw = []


class Buf:
    __slots__ = ("writer", "readers", "pend")

    def __init__(self):
        self.writer = None
        self.readers = []
        self.pend = None


def op(E, fn, reads=(), writes=(), sig=True, dma=None, selfdep=False):
    deps = {}

    def add(tok):
        if tok is None:
            return
        sc, v = tok
        if deps.get(sc, 0) < v:
            deps[sc] = v

    for b in reads:
        assert b.pend is None or b.pend is E, "pending access by other engine"
        add(b.writer)
    for b in writes:
        assert b.pend is None or b.pend is E, "pending access by other engine"
        add(b.writer)
        for t in b.readers:
            add(t)
    for sc, v in deps.items():
        if sc is E.sc and dma is None and not selfdep:
            continue
        if E.seen.get(sc, 0) < v:
            E.e.wait_ge(sc.h, v * sc.unit)
            E.seen[sc] = v
    ins = fn()
    if dma is not None:
        dma.count += 1
        ins.then_inc(dma.h, 16)
        tok = (dma, dma.count)
        for b in reads:
            b.readers.append(tok)
        for b in writes:
            b.writer = tok
            b.readers = []
        return
    E.pr.extend(reads)
    E.pw.extend(writes)
    for b in reads:
        b.pend = E
    for b in writes:
        b.pend = E
    if sig:
        E.sc.count += 1
        ins.then_inc(E.sc.h, 1)
        tok = (E.sc, E.sc.count)
        for b in E.pr:
            b.readers.append(tok)
            b.pend = None
        for b in E.pw:
            b.writer = tok
            b.readers = []
            b.pend = None
        E.pr = []
        E.pw = []


def build(debug_outs=(), upto=9):
    nc = bass.Bass("TRN2", target_bir_lowering=False)

    def dram_in(name, shape, dt=F32):
        return nc.dram_tensor(name, list(shape), dt, kind="ExternalInput").ap()

    def dram_tmp(name, shape, dt):
        kind = "ExternalOutput" if name in debug_outs else "Internal"
        return nc.dram_tensor(name, list(shape), dt, kind=kind).ap()

    xw = dram_in("xw", [TOK, D])
    w_in = dram_in("w_in", [D, 3 * D])
    w_out = dram_in("w_out", [D, D])
    w_up = dram_in("w_up", [D, 2 * DFF])
    w_down = dram_in("w_down", [DFF, D])
    g1T_d = dram_in("g1T", [128, KC])
    g2T_d = dram_in("g2T", [128, KC])
    goT_d = dram_in("goT", [128, KC])
    qkg_d = dram_in("qkg", [128, 4])
    convp_d = dram_in("convp", [128, 2 * FC, 4])
    nab_d = dram_in("nab", [16, 128, NA_T, 512])
    dlb_d = dram_in("dlb", [16, 128, DL_T, 512])
    ident_d = dram_in("ident", [128, 128])
    y = nc.dram_tensor("y", [2048, D], F32, kind="ExternalOutput").ap()

    QT_d = dram_tmp("QT_d", [32, 128, NQ], BF16)
    KT_d = dram_tmp("KT_d", [32, 128, TOK], BF16)
    V_d = dram_tmp("V_d", [TOK, D], BF16)
    mixT_d = dram_tmp("mixT_d", [D, NQ], BF16)
    x1_d = dram_tmp("x1_d", [NQ, D], F32)
    actT_d = dram_tmp("actT_d", [DFF, 2048], BF16)

    w_in_v = w_in.rearrange("(c p) n -> p c n", p=128)
    w_out_v = w_out.rearrange("(c p) n -> p c n", p=128)
    w_up_v = w_up.rearrange("(c p) n -> p c n", p=128)
    w_down_v = w_down.rearrange("(c p) n -> p c n", p=128)
    mixT_v = mixT_d.rearrange("(c p) t -> p c t", p=128)
    actT_v = actT_d.rearrange("(c p) t -> p c t", p=128)

    with ExitStack() as ges:
        G = ges.enter_context
        PE = Eng(nc, ges, nc.tensor, "pe")
        ACT = Eng(nc, ges, nc.scalar, "act")
        DVE = Eng(nc, ges, nc.vector, "dve")
        POOL = Eng(nc, ges, nc.gpsimd, "pool")
        SP = Eng(nc, ges, nc.sync, "sp")
        engines = [PE, ACT, DVE, POOL, SP]
        bar = G(nc.semaphore("bar"))
        all_sems = [e.sc for e in engines]
        nbar = [0]
        semctr = [0]

        def dsem(es):
            semctr[0] += 1
            s = SemC(nc, ges, "d%d" % semctr[0], 16)
            all_sems.append(s)
            return s

        def barrier():
            for sc in all_sems:
                if sc.count > 0 and SP.seen.get(sc, 0) < sc.count and sc is not SP.sc:
                    SP.e.wait_ge(sc.h, sc.count * sc.unit)
                    SP.seen[sc] = sc.count
            nbar[0] += 1
            SP.e.sem_inc(bar, 1)
            for e in engines:
                if e is not SP:
                    e.e.wait_ge(bar, nbar[0])
                assert not e.pr and not e.pw
                for sc in all_sems:
                    e.seen[sc] = sc.count

        PB = [G(nc.psum_tensor("pb%d" % i, [128, 512], F32)) for i in range(8)]
        PBb = [Buf() for _ in range(8)]
        ident = G(nc.sbuf_tensor("ident_sb", [128, 128], F32))
        ones = G(nc.sbuf_tensor("ones", [128, 128], BF16))
        epsT = G(nc.sbuf_tensor("epsT", [128, 1], F32))
        zeros = G(nc.sbuf_tensor("zeros", [128, 128], BF16))
        g1T = G(nc.sbuf_tensor("g1T_sb", [128, KC], F32))
        g2T = G(nc.sbuf_tensor("g2T_sb", [128, KC], F32))
        goT = G(nc.sbuf_tensor("goT_sb", [128, KC], F32))
        qkg = G(nc.sbuf_tensor("qkg_sb", [128, 4], F32))
        rstdmix = G(nc.sbuf_tensor("rstdmix", [128, 2 * NQB], F32))
        nst = G(nc.sbuf_tensor("nst", [128, 8, 4], F32))
        cB = Buf()
        nstB = [Buf() for _ in range(8)]
        rstdmixB = Buf()
        cs_ = dsem(ges)
        for dst, src in ((ident, ident_d), (g1T, g1T_d), (g2T, g2T_d), (goT, goT_d), (qkg, qkg_d)):
            op(SP, lambda dst=dst, src=src: SP.e.dma_start(out=dst[:], in_=src[:, :]), writes=[cB], dma=cs_)
        op(DVE, lambda: DVE.e.memset(ones[:], 1.0), writes=[cB], sig=False)
        op(DVE, lambda: DVE.e.memset(epsT[:], EPS), writes=[cB], sig=False)
        op(DVE, lambda: DVE.e.memset(zeros[:], 0.0), writes=[cB], sig=False)
        op(DVE, lambda: DVE.e.tensor_scalar(out=qkg[:, 0:1], in0=qkg[:, 0:1], scalar1=float(128 ** -0.5),
                                            scalar2=None, op0=ALU.mult), reads=[cB], writes=[cB], sig=False)
        op(DVE, lambda: DVE.e.tensor_scalar(out=qkg[:, 2:3], in0=qkg[:, 2:3], scalar1=float(128 ** -0.5),
                                            scalar2=None, op0=ALU.mult), reads=[cB], writes=[cB], sig=True)
        barrier()

        def norm_transpose(st, src_rows, np_, gT, dst, dstB, col0, slot):
            xt, xtB, junk, junkB, xs = st["xt"], st["xtB"], st["junk"], st["junkB"], st["xs"]
            s = slot % len(xt)
            nb = nstB[s]
            op(SP, lambda: SP.e.dma_start(out=xt[s][0:np_, :], in_=src_rows), writes=[xtB[s]], dma=xs[s])
            op(ACT, lambda: ACT.e.memzero(nst[0:np_, s, 0:1]), writes=[nb])
            op(ACT, lambda: ACT.e.activation(out=junk[0:np_, :], in_=xt[s][0:np_, :], func=AF.Square,
                                             accum_out=nst[0:np_, s, 0:1]),
               reads=[xtB[s]], writes=[junkB, nb], selfdep=True)
            op(ACT, lambda: ACT.e.activation(out=nst[0:np_, s, 1:2], in_=nst[0:np_, s, 0:1], func=AF.Ln,
                                             scale=1.0 / D, bias=epsT[0:np_, 0:1]),
               reads=[nb, cB], writes=[nb], selfdep=True)
            op(ACT, lambda: ACT.e.activation(out=nst[0:np_, s, 3:4], in_=nst[0:np_, s, 1:2], func=AF.Exp, scale=-0.5),
               reads=[nb], writes=[nb], selfdep=True)
            op(ACT, lambda: ACT.e.activation(out=xt[s][0:np_, :], in_=xt[s][0:np_, :], func=AF.Copy,
                                             scale=nst[0:np_, s, 3:4]), reads=[nb, xtB[s]], writes=[xtB[s]],
               selfdep=True)
            for q in range(8):
                bk = st["tb"][q % 2]
                for j in range(4):
                    kc = q * 4 + j
                    op(PE, lambda kc=kc, j=j, bk=bk: PE.e.transpose(
                        out=PB[bk][:, j * 128:j * 128 + np_], in_=xt[s][0:np_, kc * 128:(kc + 1) * 128],
                        identity=ident[0:np_, 0:np_]),
                       reads=[xtB[s], cB], writes=[PBb[bk]], sig=(j == 3))
                for j in range(4):
                    kc = q * 4 + j
                    op(DVE, lambda kc=kc, j=j, bk=bk: DVE.e.tensor_scalar(
                        out=dst[:, kc, col0:col0 + np_], in0=PB[bk][:, j * 128:j * 128 + np_],
                        scalar1=gT[:, kc:kc + 1], scalar2=None, op0=ALU.mult),
                       reads=[PBb[bk], cB], writes=[dstB], sig=(j == 3))

        def norm_state(es, tb, nslots=2, own_junk=True):
            st = {}
            semctr[0] += 1
            uid = semctr[0]
            st["xt"] = [es.enter_context(nc.sbuf_tensor("xt%d_%d" % (i, uid), [128, D], F32)) for i in range(nslots)]
            st["xtB"] = [Buf() for _ in range(nslots)]
            if own_junk:
                st["junk"] = es.enter_context(nc.sbuf_tensor("junk%d" % uid, [128, D], BF16))
                st["junkB"] = Buf()
            st["xs"] = [dsem(es) for _ in range(nslots)]
            st["tb"] = tb
            return st

        if upto < 2:
            return nc
        SGS = [(0, 8), (8, 16), (16, 25)]
        need_hi = {0: 17, 1: 19, 2: 19, 3: 17, 4: 25, 5: 25}
        with ExitStack() as es:
            E_ = es.enter_context
            st = norm_state(es, (6, 7), nslots=3, own_junk=False)
            hT = E_(nc.sbuf_tensor("hT", [128, KC, 9 * 128], BF16))
            hTB = [Buf() for _ in range(9)]
            wsl = [E_(nc.sbuf_tensor("wsl%d" % i, [128, KC, 512], BF16)) for i in range(2)]
            wslB = [Buf(), Buf()]
            wsem = [dsem(es), dsem(es)]
            sq = [E_(nc.sbuf_tensor("sq%d" % i, [128, 512], BF16)) for i in range(2)]
            sqB = [Buf(), Buf()]
            rt = [E_(nc.sbuf_tensor("rt%d" % i, [128, 512], F32)) for i in range(2)]
            rtB = [Buf(), Buf()]
            ri = [E_(nc.sbuf_tensor("ri%d" % i, [128, 512], F32)) for i in range(2)]
            riB = [Buf(), Buf()]
            og = [E_(nc.sbuf_tensor("og%d" % i, [128, 512], BF16)) for i in range(3)]
            ogB = [Buf() for _ in range(3)]
            osem = [dsem(es) for _ in range(3)]
            ABK = (0, 1, 2)
            BBK = (3, 4)
            cnt = {"a": 0, "b": 0, "o": 0, "w": 0, "u": 0, "x": 0}
            pending_ones = []

            def flush_ones():
                while pending_ones:
                    pending_ones.pop(0)()

            for (b0, b1) in SGS:
                jslot = (cnt["w"] + 1) % 2
                st["junk"] = wsl[jslot][:, 0:8, :].rearrange("p a b -> p (a b)")
                st["junkB"] = wslB[jslot]
                for b in range(b0, b1):
                    norm_transpose(st, xw[b * 128:(b + 1) * 128, :], 128, g1T, hT, hTB[b - b0], (b - b0) * 128,
                                   cnt["x"])
                    cnt["x"] += 1
                slab_order = list(range(24))
                if b0 >= 16:
                    others = [0, 1, 2, 3, 12, 13, 14, 15, 4, 5, 6, 7, 8, 9, 10, 11]
                    slab_order = []
                    for i in range(8):
                        slab_order += [16 + i, others[2 * i], others[2 * i + 1]]
                for slab in slab_order:
                    typ = slab // 4
                    hi = min(b1, need_hi[typ])
                    if hi <= b0:
                        continue
                    ws = cnt["w"] % 2
                    cnt["w"] += 1
                    op(POOL, lambda ws=ws, slab=slab: POOL.e.dma_start(
                        out=wsl[ws][:], in_=w_in_v[:, :, slab * 512:(slab + 1) * 512]),
                       writes=[wslB[ws]], dma=wsem[ws])
                    if typ in (2, 5):
                        vc0 = (slab - 8) * 512 if typ == 2 else 2048 + (slab - 20) * 512
                        for b in range(b0, hi):
                            a = ABK[cnt["a"] % 3]
                            cnt["a"] += 1
                            lb = b - b0
                            for kc in range(KC):
                                op(PE, lambda kc=kc, a=a, lb=lb, ws=ws: PE.e.matmul(
                                    PB[a][:, :], lhsT=hT[:, kc, lb * 128:(lb + 1) * 128], rhs=wsl[ws][:, kc, :],
                                    start=(kc == 0), stop=(kc == KC - 1)),
                                   reads=[hTB[lb], wslB[ws]], writes=[PBb[a]], sig=(kc == KC - 1))
                            flush_ones()
                            o = cnt["o"] % 3
                            cnt["o"] += 1
                            op(ACT, lambda a=a, o=o: ACT.e.activation(out=og[o][:, :], in_=PB[a][:, :], func=AF.Copy),
                               reads=[PBb[a]], writes=[ogB[o]])
                            op(SP, lambda o=o, b=b, vc0=vc0: SP.e.dma_start(
                                out=V_d[b * 128:(b + 1) * 128, vc0:vc0 + 512], in_=og[o][:, :]),
                               reads=[ogB[o]], dma=osem[o])
                    else:
                        isq = typ in (0, 3)
                        gcol = {0: 0, 1: 1, 3: 2, 4: 3}[typ]
                        hbase = {0: 0, 1: 0, 3: 16, 4: 16}[typ] + (slab % 4) * 4
                        dstT = QT_d if isq else KT_d
                        groups = []
                        bb = b0
                        while bb < hi:
                            e2 = min(hi, bb + 4)
                            groups.append((bb, e2))
                            bb = e2
                        for hh in range(4):
                            head = hbase + hh
                            for (gb0, gb1) in groups:
                                n = (gb1 - gb0) * 128
                                c0 = (gb0 - b0) * 128
                                a = ABK[cnt["a"] % 3]
                                cnt["a"] += 1
                                bk = BBK[cnt["b"] % 2]
                                u = cnt["u"] % 2
                                cnt["b"] += 1
                                cnt["u"] += 1
                                rb = [hTB[i - b0] for i in range(gb0, gb1)]
                                for kc in range(KC):
                                    op(PE, lambda kc=kc, a=a, ws=ws, hh=hh, c0=c0, n=n: PE.e.matmul(
                                        PB[a][:, 0:n], lhsT=wsl[ws][:, kc, hh * 128:(hh + 1) * 128],
                                        rhs=hT[:, kc, c0:c0 + n], start=(kc == 0), stop=(kc == KC - 1)),
                                       reads=rb + [wslB[ws]], writes=[PBb[a]], sig=(kc == KC - 1))
                                flush_ones()
                                op(ACT, lambda a=a, u=u, n=n: ACT.e.activation(
                                    out=sq[u][:, 0:n], in_=PB[a][:, 0:n], func=AF.Square),
                                   reads=[PBb[a]], writes=[sqB[u]])

                                def ones_mm(a=a, bk=bk, u=u, n=n, head=head, gb0=gb0, gcol=gcol, dstT=dstT):
                                    op(PE, lambda: PE.e.matmul(PB[bk][:, 0:n], lhsT=ones[:, :], rhs=sq[u][:, 0:n],
                                                               start=True, stop=True),
                                       reads=[sqB[u], cB], writes=[PBb[bk]])
                                    op(ACT, lambda: ACT.e.activation(out=rt[u][:, 0:n], in_=PB[bk][:, 0:n],
                                                                     func=AF.Sqrt, scale=1.0 / 128, bias=epsT[:, 0:1]),
                                       reads=[PBb[bk], cB], writes=[rtB[u]])
                                    op(DVE, lambda: DVE.e.reciprocal(out=ri[u][:, 0:n], in_=rt[u][:, 0:n]),
                                       reads=[rtB[u]], writes=[riB[u]], sig=False)
                                    o = cnt["o"] % 3
                                    cnt["o"] += 1
                                    op(DVE, lambda: DVE.e.scalar_tensor_tensor(
                                        out=og[o][:, 0:n], in0=PB[a][:, 0:n], scalar=qkg[:, gcol:gcol + 1],
                                        in1=ri[u][:, 0:n], op0=ALU.mult, op1=ALU.mult),
                                       reads=[PBb[a], riB[u], cB], writes=[ogB[o]])
                                    op(SP, lambda: SP.e.dma_start(out=dstT[head, :, gb0 * 128:gb0 * 128 + n],
                                                                  in_=og[o][:, 0:n]),
                                       reads=[ogB[o]], dma=osem[o])

                                pending_ones.append(ones_mm)
                flush_ones()
            barrier()

        if upto < 3:
            return nc
        with ExitStack() as es:
            E_ = es.enter_context
            qt = [E_(nc.sbuf_tensor("qt%d" % i, [128, NQ], BF16)) for i in range(2)]
            kt = [E_(nc.sbuf_tensor("kt%d" % i, [128, TOK], BF16)) for i in range(2)]
            bt = [E_(nc.sbuf_tensor("bt%d" % i, [128, DL_T, 512], F32)) for i in range(2)]
            vq = [E_(nc.sbuf_tensor("vq%d" % i, [128, NBLK, 512], BF16)) for i in range(2)]
            qB = [Buf(), Buf()]
            kB = [Buf(), Buf()]
            bB = [Buf(), Buf()]
            vqB = [Buf(), Buf()]
            qsem = [dsem(es), dsem(es)]
            ksem = [dsem(es), dsem(es)]
            bsem = [dsem(es), dsem(es)]
            vsem = [dsem(es), dsem(es)]
            NSL = 8
            tt = [E_(nc.sbuf_tensor("tt%d" % i, [128, 512], F32)) for i in range(NSL)]
            ttB = [Buf() for _ in range(NSL)]
            pp = [E_(nc.sbuf_tensor("pp%d" % i, [128, 512], BF16)) for i in range(NSL)]
            ppB = [Buf() for _ in range(NSL)]
            rl = [E_(nc.sbuf_tensor("rl%d" % i, [128, 512], F32)) for i in range(2)]
            on = [E_(nc.sbuf_tensor("on%d" % i, [128, 512], F32)) for i in range(2)]
            sq3 = [E_(nc.sbuf_tensor("sqa%d" % i, [128, 512], BF16)) for i in range(2)]
            mx = [E_(nc.sbuf_tensor("mx%d" % i, [128, 512], BF16)) for i in range(4)]
            rlB = [Buf(), Buf()]
            onB = [Buf(), Buf()]
            sq3B = [Buf(), Buf()]
            mxB = [Buf() for _ in range(4)]
            msem = [dsem(es) for _ in range(4)]
            SBK = (0, 1, 2)
            OBK = (3, 5)
            LBK = (4, 6)
            SSB = 7
            LOOK = 3
            op(DVE, lambda: DVE.e.memset(PB[SSB][:, :], 0.0), writes=[PBb[SSB]])
            na_rng, dl_rng = _col_ranges()
            ci = 0
            gi = 0
            fin_prev = None
            epi_prev = None
            def load_vq(hq):
                vs = hq % 2
                nvb = 19 if hq < 4 else NBLK
                op(SP, lambda: SP.e.dma_start(
                    out=vq[vs][:, 0:nvb, :],
                    in_=V_d[0:nvb * 128, hq * 512:(hq + 1) * 512].rearrange("(k p) c -> p k c", p=128)),
                   writes=[vqB[vs]], dma=vsem[vs])

            def load_head(h):
                na = h < 16
                hs = h % 2
                nkb = 19 if na else NBLK
                ntile = NA_T if na else DL_T
                bsrc = nab_d[h] if na else dlb_d[h - 16]
                op(SP, lambda: SP.e.dma_start(out=qt[hs][:, :], in_=QT_d[h, :, :]),
                   writes=[qB[hs]], dma=qsem[hs])
                op(SP, lambda: SP.e.dma_start(out=kt[hs][:, 0:nkb * 128], in_=KT_d[h, :, 0:nkb * 128]),
                   writes=[kB[hs]], dma=ksem[hs])
                op(SP, lambda: SP.e.dma_start(out=bt[hs][:, 0:ntile, :], in_=bsrc),
                   writes=[bB[hs]], dma=bsem[hs])

            mxc = [0]

            def _epiA(gs, nq, ob, lbk, h, q0):
                op(ACT, lambda: ACT.e.activation(out=rl[gs][:, 0:nq], in_=PB[lbk][:, 0:nq], func=AF.Ln),
                   reads=[PBb[lbk]], writes=[rlB[gs]], sig=False)
                op(ACT, lambda: ACT.e.activation(out=rl[gs][:, 0:nq], in_=rl[gs][:, 0:nq], func=AF.Exp, scale=-1.0),
                   reads=[rlB[gs]], writes=[rlB[gs]])

            def _epiB(gs, nq, ob, lbk, h, q0):
                op(DVE, lambda: DVE.e.tensor_tensor(out=on[gs][:, 0:nq], in0=PB[ob][:, 0:nq],
                                                    in1=rl[gs][:, 0:nq], op=ALU.mult),
                   reads=[PBb[ob], rlB[gs]], writes=[onB[gs]])
                op(DVE, lambda: DVE.e.tensor_tensor(out=sq3[gs][:, 0:nq], in0=on[gs][:, 0:nq],
                                                    in1=on[gs][:, 0:nq], op=ALU.mult),
                   reads=[onB[gs]], writes=[sq3B[gs]])

            def _epiC(gs, nq, ob, lbk, h, q0):
                ms_ = mxc[0] % 4
                mxc[0] += 1
                op(ACT, lambda: ACT.e.activation(out=mx[ms_][:, 0:nq], in_=on[gs][:, 0:nq], func=AF.Copy,
                                                 scale=goT[:, h:h + 1]),
                   reads=[onB[gs], cB], writes=[mxB[ms_]])
                op(SP, lambda: SP.e.dma_start(out=mixT_d[h * 128:(h + 1) * 128, q0:q0 + nq], in_=mx[ms_][:, 0:nq]),
                   reads=[mxB[ms_]], dma=msem[ms_])

            load_vq(0)
            load_head(0)
            for hq in range(8):
                vs = hq % 2
                for hh in range(4):
                    h = hq * 4 + hh
                    na = h < 16
                    hs = h % 2
                    if h + 1 < 32:
                        load_head(h + 1)
                    if hh == 0 and hq + 1 < 8:
                        load_vq(hq + 1)
                    first = (h % 16 == 0)
                    last = (h % 16 == 15)
                    typ = 0 if na else 1
                    for g in range(5):
                        nq = 512 if g < 4 else 128
                        q0 = g * 512
                        if na:
                            if g == 0:
                                kbs = [(kb, 8 + kb) for kb in range(0, 6)]
                            elif g < 4:
                                kbs = [(kb, kb - 4 * g + 2) for kb in range(4 * g - 2, 4 * g + 6)]
                            else:
                                kbs = [(kb, kb - 16 + 2) for kb in range(14, 19)]
                        else:
                            if g < 4:
                                kbs = [(kb, kb - 4 * g + 8) for kb in range(max(0, 4 * g - 8), min(24, 4 * g + 11) + 1)]
                            else:
                                kbs = [(kb, kb - 16 + 8) for kb in range(8, 25)]
                        ob = OBK[gi % 2]
                        lbk = LBK[gi % 2]
                        gs = gi % 2
                        gi += 1
                        nk = len(kbs)
                        slots = []

                        crng = na_rng if na else dl_rng
                        full = [i for i, (_, t) in enumerate(kbs) if crng[t][0] == 0 and crng[t][1] >= nq]
                        if full:
                            kbs.insert(0, kbs.pop(full[0]))
                        cr = [(0, nq) if i == 0 else (min(crng[t][0], nq), min(crng[t][1], nq))
                              for i, (_, t) in enumerate(kbs)]
                        assert all(b_ > a_ for a_, b_ in cr)

                        def emit_S(i):
                            nonlocal ci
                            kb, tid = kbs[i]
                            a_, b_ = cr[i]
                            sl = ci % NSL
                            sb = SBK[ci % len(SBK)]
                            ci += 1
                            slots.append(sl)
                            op(PE, lambda: PE.e.matmul(PB[sb][:, a_:b_], lhsT=kt[hs][:, kb * 128:(kb + 1) * 128],
                                                       rhs=qt[hs][:, q0 + a_:q0 + b_], start=True, stop=True),
                               reads=[qB[hs], kB[hs]], writes=[PBb[sb]])
                            op(DVE, lambda: DVE.e.tensor_tensor(out=tt[sl][:, a_:b_], in0=PB[sb][:, a_:b_],
                                                                in1=bt[hs][:, tid, a_:b_], op=ALU.add),
                               reads=[PBb[sb], bB[hs]], writes=[ttB[sl]])
                            op(ACT, lambda: ACT.e.activation(out=pp[sl][:, a_:b_], in_=tt[sl][:, a_:b_], func=AF.Exp),
                               reads=[ttB[sl]], writes=[ppB[sl]])

                        def emit_PV(i):
                            kb, tid = kbs[i]
                            a_, b_ = cr[i]
                            sl = slots[i]
                            op(PE, lambda: PE.e.matmul(PB[ob][:, a_:b_], lhsT=vq[vs][:, kb, hh * 128:(hh + 1) * 128],
                                                       rhs=pp[sl][:, a_:b_], start=(i == 0), stop=(i == nk - 1),
                                                       skip_group_check=True),
                               reads=[vqB[vs], ppB[sl]], writes=[PBb[ob]], sig=False)
                            op(PE, lambda: PE.e.matmul(PB[lbk][:, a_:b_], lhsT=ones[:, :], rhs=pp[sl][:, a_:b_],
                                                       start=(i == 0), stop=(i == nk - 1), skip_group_check=True),
                               reads=[cB, ppB[sl]], writes=[PBb[lbk]], sig=True)

                        for i in range(min(LOOK, nk)):
                            emit_S(i)
                        for i in range(nk):
                            if i + LOOK < nk:
                                emit_S(i + LOOK)
                            emit_PV(i)
                            if epi_prev is not None:
                                if i == 0:
                                    _epiA(*epi_prev)
                                if i == min(2, nk - 1):
                                    _epiB(*epi_prev)
                                if i == min(4, nk - 1):
                                    _epiC(*epi_prev)
                                    fin_prev()
                                    fin_prev = None
                                    epi_prev = None

                        epi_next = (gs, nq, ob, lbk, h, q0)
                        def fin(gs=gs, nq=nq, g=g, typ=typ, first=first, last=last):
                            nt = nq // 128
                            for j in range(nt):
                                col = typ * NQB + g * 4 + j
                                op(PE, lambda: PE.e.matmul(PB[SSB][:, col:col + 1], lhsT=sq3[gs][:, j * 128:(j + 1) * 128],
                                                           rhs=ones[:, 0:1], start=False, stop=False,
                                                           skip_group_check=True),
                                   reads=[sq3B[gs], cB], writes=[PBb[SSB]], sig=(j == nt - 1))

                        fin_prev = fin
                        epi_prev = epi_next
            _epiA(*epi_prev)
            _epiB(*epi_prev)
            _epiC(*epi_prev)
            fin_prev()
            op(DVE, lambda: DVE.e.tensor_scalar(out=rstdmix[:, :], in0=PB[SSB][:, 0:2 * NQB], scalar1=1.0 / 2048,
                                                scalar2=EPS, op0=ALU.mult, op1=ALU.add),
               reads=[PBb[SSB]], writes=[rstdmixB])
            op(ACT, lambda: ACT.e.activation(out=rstdmix[:, :], in_=rstdmix[:, :], func=AF.Sqrt),
               reads=[rstdmixB], writes=[rstdmixB])
            op(DVE, lambda: DVE.e.reciprocal(out=rstdmix[:, :], in_=rstdmix[:, :]), reads=[rstdmixB], writes=[rstdmixB])
            barrier()

        if upto < 4:
            return nc
        with ExitStack() as es:
            E_ = es.enter_context
            mT = E_(nc.sbuf_tensor("mT", [128, KC, 9 * 128], BF16))
            mTB = [Buf() for _ in range(3)]
            mTs = [dsem(es) for _ in range(3)]
            wsl = [E_(nc.sbuf_tensor("wo%d" % i, [128, KC, 512], BF16)) for i in range(2)]
            wslB = [Buf(), Buf()]
            wsem = [dsem(es), dsem(es)]
            xs_ = [E_(nc.sbuf_tensor("xs%d" % i, [128, 512], F32)) for i in range(3)]
            xsB = [Buf() for _ in range(3)]
            xsem = [dsem(es) for _ in range(3)]
            o1 = [E_(nc.sbuf_tensor("o1%d" % i, [128, 512], F32)) for i in range(3)]
            o1B = [Buf() for _ in range(3)]
            o2 = [E_(nc.sbuf_tensor("o2%d" % i, [128, 512], F32)) for i in range(3)]
            o2B = [Buf() for _ in range(3)]
            o2sem = [dsem(es) for _ in range(3)]
            u = 0
            w = 0
            for (b0, b1) in ((0, 9), (9, 17)):
                nt = (b1 - b0) * 128
                for pi in range(3):
                    c0 = pi * 384
                    c1 = min(nt, c0 + 384)
                    op(SP, lambda b0=b0, c0=c0, c1=c1: SP.e.dma_start(
                        out=mT[:, :, c0:c1], in_=mixT_v[:, :, b0 * 128 + c0:b0 * 128 + c1]),
                       writes=[mTB[pi]], dma=mTs[pi])
                for cs in range(8):
                    ws = w % 2
                    w += 1
                    op(POOL, lambda ws=ws, cs=cs: POOL.e.dma_start(out=wsl[ws][:], in_=w_out_v[:, :, cs * 512:(cs + 1) * 512]),
                       writes=[wslB[ws]], dma=wsem[ws])
                    for b in range(b0, b1):
                        lb = b - b0
                        k3 = u % 3
                        pa = (0, 2, 4)[k3]
                        pb_ = (1, 3, 5)[k3]
                        u += 1
                        op(SP, lambda k3=k3, b=b, cs=cs: SP.e.dma_start(
                            out=xs_[k3][:, :], in_=xw[b * 128:(b + 1) * 128, cs * 512:(cs + 1) * 512]),
                           writes=[xsB[k3]], dma=xsem[k3])
                        for kc in range(16):
                            op(PE, lambda kc=kc, pa=pa, lb=lb, ws=ws: PE.e.matmul(
                                PB[pa][:, :], lhsT=mT[:, kc, lb * 128:(lb + 1) * 128], rhs=wsl[ws][:, kc, :],
                                start=(kc == 0), stop=(kc == 15)),
                               reads=[mTB[lb // 3], wslB[ws]], writes=[PBb[pa]], sig=(kc == 15))
                        for kc in range(16, 32):
                            op(PE, lambda kc=kc, pb_=pb_, lb=lb, ws=ws: PE.e.matmul(
                                PB[pb_][:, :], lhsT=mT[:, kc, lb * 128:(lb + 1) * 128], rhs=wsl[ws][:, kc, :],
                                start=(kc == 16), stop=(kc == 31)),
                               reads=[mTB[lb // 3], wslB[ws]], writes=[PBb[pb_]], sig=(kc == 31))
                        op(DVE, lambda k3=k3, pa=pa, b=b: DVE.e.scalar_tensor_tensor(
                            out=o1[k3][:, :], in0=PB[pa][:, :], scalar=rstdmix[:, b:b + 1], in1=xs_[k3][:, :],
                            op0=ALU.mult, op1=ALU.add),
                           reads=[PBb[pa], rstdmixB, xsB[k3]], writes=[o1B[k3]], sig=False)
                        op(DVE, lambda k3=k3, pb_=pb_, b=b: DVE.e.scalar_tensor_tensor(
                            out=o2[k3][:, :], in0=PB[pb_][:, :], scalar=rstdmix[:, NQB + b:NQB + b + 1], in1=o1[k3][:, :],
                            op0=ALU.mult, op1=ALU.add),
                           reads=[PBb[pb_], rstdmixB, o1B[k3]], writes=[o2B[k3]])
                        op(SP, lambda k3=k3, b=b, cs=cs: SP.e.dma_start(
                            out=x1_d[b * 128:(b + 1) * 128, cs * 512:(cs + 1) * 512], in_=o2[k3][:, :]),
                           reads=[o2B[k3]], dma=o2sem[k3])
            barrier()

        if upto < 5:
            return nc
        with ExitStack() as es:
            E_ = es.enter_context
            st = norm_state(es, (6, 7), nslots=2, own_junk=False)
            NCOL = 1026
            GW = 342
            h2 = E_(nc.sbuf_tensor("h2", [128, KC, NCOL], BF16))
            h2B = [Buf() for _ in range(10)]
            cvp = E_(nc.sbuf_tensor("cvp", [128, 2 * FC, 4], F32))
            cvs = dsem(es)
            op(SP, lambda: SP.e.dma_start(out=cvp[:], in_=convp_d[:, :, :]), writes=[cB], dma=cvs)
            wg = [E_(nc.sbuf_tensor("wg%d" % i, [128, KC, 256], BF16)) for i in range(2)]
            wu = [E_(nc.sbuf_tensor("wu%d" % i, [128, KC, 256], BF16)) for i in range(2)]
            wgB = [Buf(), Buf()]
            wuB = [Buf(), Buf()]
            wgs = [dsem(es), dsem(es)]
            wus = [dsem(es), dsem(es)]
            ub = [[E_(nc.sbuf_tensor("ub%d_%d" % (t, i), [128, NCOL], F32)) for i in range(2)] for t in range(2)]
            ubB = [[Buf(), Buf()], [Buf(), Buf()]]
            cgt = [E_(nc.sbuf_tensor("cg%d" % i, [128, 1024], F32)) for i in range(2)]
            cut = [E_(nc.sbuf_tensor("cu%d" % i, [128, 1024], F32)) for i in range(2)]
            cgB = [Buf(), Buf()]
            cuB = [Buf(), Buf()]
            ab = [E_(nc.sbuf_tensor("ab%d" % i, [128, 1024], BF16)) for i in range(2)]
            abB = [Buf(), Buf()]
            absem = [dsem(es), dsem(es)]
            xcnt = 0
            wcnt = 0
            fcnt = 0
            ucnt = 0
            for sg in range(2):
                t0 = sg * 1024
                jslot = (wcnt + 1) % 2
                st["junk"] = wu[jslot][:, 0:16, :].rearrange("p a b -> p (a b)")
                st["junkB"] = wuB[jslot]
                if sg == 0:
                    op(DVE, lambda: DVE.e.memset(h2[:, :, 0:1], 0.0), writes=[h2B[0]])
                else:
                    norm_transpose(st, x1_d[t0 - 1:t0, :], 1, g2T, h2, h2B[0], 0, xcnt)
                    xcnt += 1
                for i in range(8):
                    b = sg * 8 + i
                    norm_transpose(st, x1_d[b * 128:(b + 1) * 128, :], 128, g2T, h2, h2B[1 + i], 1 + i * 128, xcnt)
                    xcnt += 1
                norm_transpose(st, x1_d[t0 + 1024:t0 + 1025, :], 1, g2T, h2, h2B[9], 1025, xcnt)
                xcnt += 1
                for fq in range(43):
                    ws = wcnt % 2
                    wcnt += 1
                    op(POOL, lambda ws=ws, fq=fq: POOL.e.dma_start(out=wg[ws][:], in_=w_up_v[:, :, fq * 256:(fq + 1) * 256]),
                       writes=[wgB[ws]], dma=wgs[ws])
                    op(POOL, lambda ws=ws, fq=fq: POOL.e.dma_start(
                        out=wu[ws][:], in_=w_up_v[:, :, DFF + fq * 256:DFF + (fq + 1) * 256]),
                       writes=[wuB[ws]], dma=wus[ws])
                    for fc in range(2):
                        f = fq * 2 + fc
                        fs = fcnt % 2
                        fcnt += 1
                        for typ in range(2):
                            wt, wtB = (wg, wgB) if typ == 0 else (wu, wuB)
                            bks = (0, 1, 2) if ucnt % 2 == 0 else (3, 4, 5)
                            ucnt += 1
                            for kc in range(KC):
                                for gr in range(3):
                                    op(PE, lambda kc=kc, gr=gr, wt=wt, ws=ws, fc=fc, bks=bks: PE.e.matmul(
                                        PB[bks[gr]][:, 0:GW], lhsT=wt[ws][:, kc, fc * 128:(fc + 1) * 128],
                                        rhs=h2[:, kc, gr * GW:(gr + 1) * GW], start=(kc == 0), stop=(kc == KC - 1)),
                                       reads=h2B + [wtB[ws]], writes=[PBb[bks[gr]]],
                                       sig=(kc == KC - 1 and gr == 2))
                            for gr in range(3):
                                op(ACT, lambda gr=gr, typ=typ, fs=fs, bks=bks: ACT.e.activation(
                                    out=ub[typ][fs][:, gr * GW:(gr + 1) * GW], in_=PB[bks[gr]][:, 0:GW], func=AF.Copy),
                                   reads=[PBb[bks[gr]]], writes=[ubB[typ][fs]], sig=(gr == 2))
                        for typ in range(2):
                            ch = f if typ == 0 else FC + f
                            ct, ctB = (cgt, cgB) if typ == 0 else (cut, cuB)
                            uu = ub[typ][fs]
                            uB = ubB[typ][fs]
                            op(DVE, lambda ct=ct, uu=uu, ch=ch, fs=fs: DVE.e.tensor_scalar(
                                out=ct[fs][:, :], in0=uu[:, 1:1025], scalar1=cvp[:, ch, 1:2], scalar2=cvp[:, ch, 3:4],
                                op0=ALU.mult, op1=ALU.add), reads=[uB, cB], writes=[ctB[fs]], sig=False)
                            op(DVE, lambda ct=ct, uu=uu, ch=ch, fs=fs: DVE.e.scalar_tensor_tensor(
                                out=ct[fs][:, :], in0=uu[:, 0:1024], scalar=cvp[:, ch, 0:1], in1=ct[fs][:, :],
                                op0=ALU.mult, op1=ALU.add), reads=[uB, cB, ctB[fs]], writes=[ctB[fs]], sig=False)
                            op(DVE, lambda ct=ct, uu=uu, ch=ch, fs=fs: DVE.e.scalar_tensor_tensor(
                                out=ct[fs][:, :], in0=uu[:, 2:1026], scalar=cvp[:, ch, 2:3], in1=ct[fs][:, :],
                                op0=ALU.mult, op1=ALU.add), reads=[uB, cB, ctB[fs]], writes=[ctB[fs]], sig=True)
                        op(ACT, lambda fs=fs: ACT.e.activation(out=cgt[fs][:, :], in_=cgt[fs][:, :], func=AF.Silu),
                           reads=[cgB[fs]], writes=[cgB[fs]])
                        op(DVE, lambda fs=fs: DVE.e.tensor_tensor(out=ab[fs][:, :], in0=cgt[fs][:, :], in1=cut[fs][:, :],
                                                                  op=ALU.mult),
                           reads=[cgB[fs], cuB[fs]], writes=[abB[fs]])
                        op(SP, lambda fs=fs, f=f, t0=t0: SP.e.dma_start(
                            out=actT_d[f * 128:(f + 1) * 128, t0:t0 + 1024], in_=ab[fs][:, :]),
                           reads=[abB[fs]], dma=absem[fs])
            barrier()

        if upto < 6:
            return nc
        with ExitStack() as es:
            E_ = es.enter_context
            aT = E_(nc.sbuf_tensor("aT", [128, FC, 512], BF16))
            aTB = [Buf() for _ in range(11)]
            aTs = [dsem(es) for _ in range(11)]
            wd = [E_(nc.sbuf_tensor("wd%d" % i, [128, 8, 512], BF16)) for i in range(3)]
            wdB = [Buf() for _ in range(3)]
            wds = [dsem(es) for _ in range(3)]
            x1s = [E_(nc.sbuf_tensor("x1s%d" % i, [128, 512], F32)) for i in range(3)]
            x1B = [Buf() for _ in range(3)]
            x1sem = [dsem(es) for _ in range(3)]
            ys = [E_(nc.sbuf_tensor("ys%d" % i, [128, 512], F32)) for i in range(3)]
            ysB = [Buf() for _ in range(3)]
            ysem = [dsem(es) for _ in range(3)]
            wcnt = 0
            ccnt = 0
            ycnt = 0
            pieces = [(p * 8, min(FC, p * 8 + 8)) for p in range(11)]
            def load_aT(sg):
                t0 = sg * 512
                for pi, (k0, k1) in enumerate(pieces):
                    op(SP, lambda k0=k0, k1=k1: SP.e.dma_start(out=aT[:, k0:k1, :], in_=actT_v[:, k0:k1, t0:t0 + 512]),
                       writes=[aTB[pi]], dma=aTs[pi])

            load_aT(0)
            for sg in range(4):
                t0 = sg * 512
                for cs in range(8):
                    bks = (0, 1, 2, 3) if ccnt % 2 == 0 else (4, 5, 6, 7)
                    ccnt += 1
                    for pi, (k0, k1) in enumerate(pieces):
                        ws = wcnt % 3
                        wcnt += 1
                        op(POOL, lambda ws=ws, k0=k0, k1=k1, cs=cs: POOL.e.dma_start(
                            out=wd[ws][:, 0:k1 - k0, :], in_=w_down_v[:, k0:k1, cs * 512:(cs + 1) * 512]),
                           writes=[wdB[ws]], dma=wds[ws])
                        for t in range(4):
                            for kc in range(k0, k1):
                                op(PE, lambda t=t, kc=kc, ws=ws, k0=k0, bks=bks: PE.e.matmul(
                                    PB[bks[t]][:, :], lhsT=aT[:, kc, t * 128:(t + 1) * 128], rhs=wd[ws][:, kc - k0, :],
                                    start=(kc == 0), stop=(kc == FC - 1)),
                                   reads=[aTB[pi], wdB[ws]], writes=[PBb[bks[t]]],
                                   sig=(kc == k1 - 1 and (t == 3 or kc == FC - 1)))
                    if cs == 7 and sg + 1 < 4:
                        load_aT(sg + 1)
                    for t in range(4):
                        k3 = ycnt % 3
                        ycnt += 1
                        r0 = t0 + t * 128
                        op(SP, lambda k3=k3, r0=r0, cs=cs: SP.e.dma_start(
                            out=x1s[k3][:, :], in_=x1_d[r0:r0 + 128, cs * 512:(cs + 1) * 512]),
                           writes=[x1B[k3]], dma=x1sem[k3])
                        op(DVE, lambda k3=k3, t=t, bks=bks: DVE.e.tensor_tensor(
                            out=ys[k3][:, :], in0=PB[bks[t]][:, :], in1=x1s[k3][:, :], op=ALU.add),
                           reads=[PBb[bks[t]], x1B[k3]], writes=[ysB[k3]])
                        op(SP, lambda k3=k3, r0=r0, cs=cs: SP.e.dma_start(
                            out=y[r0:r0 + 128, cs * 512:(cs + 1) * 512], in_=ys[k3][:, :]),
                           reads=[ysB[k3]], dma=ysem[k3])
            barrier()
    return nc


def _dil_bias():
    kp = np.arange(128)[:, None, None]
    tid = np.arange(DL_T)[None, :, None]
    qi = np.arange(512)[None, None, :]
    d = (tid - 8) * 128 + kp - qi
    ad = np.abs(d)
    mult = (ad <= 64).astype(np.float64) + ((d % 4 == 0) & (ad <= 256)) + ((d % 16 == 0) & (ad <= 1024))
    slopes = np.exp2(-8.0 * np.arange(1, 17, dtype=np.float64) / 16)
    out = np.empty((16, 128, DL_T, 512), np.float32)
    lm = np.log(np.maximum(mult, 1.0))
    for h in range(16):
        out[h] = np.where(mult > 0, -slopes[h] * ad + lm, NEG).astype(np.float32)
    return out


def _na_struct(half):
    kp = np.arange(128)[:, None, None]
    tid = np.arange(NA_T)[None, :, None]
    qi = np.arange(512)[None, None, :]
    qloc = np.where(tid < 8, 512 + qi, qi) + 0 * kp
    kloc = np.where(tid < 8, (2 + tid) * 128 + kp, (tid - 8) * 128 + kp) + 0 * qi
    if half == 0:
        qg, kg = qloc, kloc
    else:
        qg, kg = 4095 - qloc, 4095 - kloc
    qr, qc = qg // 64, qg % 64
    kr, kc = kg // 64, kg % 64
    r0 = np.clip(qr - 4, 0, 56)
    c0 = np.clip(qc - 8, 0, 48)
    ok = (kr >= r0) & (kr < r0 + 8) & (kc >= c0) & (kc < c0 + 16) & (kg >= 0) & (kg < 4096)
    dr = np.clip(kr - qr + 7, 0, 14)
    dc = np.clip(kc - qc + 15, 0, 30)
    return ok, dr, dc


def _na_bias(rel_bias, half):
    ok, dr, dc = _na_struct(half)
    out = np.empty((16, 128, NA_T, 512), np.float32)
    for h in range(16):
        out[h] = np.where(ok, rel_bias[h][dr, dc], np.float32(NEG))
    return out


def _col_ranges():
    def rng(anyok):
        out = []
        for t in range(anyok.shape[0]):
            idx = np.nonzero(anyok[t])[0]
            out.append((int(idx[0]) // 128 * 128, (int(idx[-1]) // 128 + 1) * 128))
        return out
    na_any = (_na_struct(0)[0] | _na_struct(1)[0]).any(axis=0)
    kp = np.arange(128)[:, None, None]
    tid = np.arange(DL_T)[None, :, None]
    qi = np.arange(512)[None, None, :]
    d = (tid - 8) * 128 + kp - qi
    ad = np.abs(d)
    dl_any = ((ad <= 64) | ((d % 4 == 0) & (ad <= 256)) | ((d % 16 == 0) & (ad <= 1024))).any(axis=0)
    return rng(na_any), rng(dl_any)


_NC_CACHE = {}


def kernel(x, norm1_g, w_in, qn_na, kn_na, qn_dil, kn_dil, rel_bias, out_norm_g, w_out, norm2_g,
           w_up, conv_w, conv_b, w_down, _debug_outs=()):
    x = np.asarray(x, np.float32)
    f32c = lambda a: np.ascontiguousarray(np.asarray(a, np.float32))
    tT = lambda g: f32c(np.asarray(g, np.float32).reshape(KC, 128).T)
    shared = {
        "w_in": f32c(w_in[0]), "w_out": f32c(w_out[0]), "w_up": f32c(w_up[0]), "w_down": f32c(w_down[0]),
        "g1T": tT(norm1_g[0]), "g2T": tT(norm2_g[0]), "goT": tT(out_norm_g[0]),
        "qkg": f32c(np.stack([qn_na[0], kn_na[0], qn_dil[0], kn_dil[0]], axis=1)),
        "dlb": _dil_bias(), "ident": np.eye(128, dtype=np.float32),
    }
    rb = np.asarray(rel_bias[0], np.float32)
    nab = [_na_bias(rb, 0), _na_bias(rb, 1)]
    cw = np.asarray(conv_w[0], np.float32)
    cb = np.asarray(conv_b[0], np.float32)
    convp = []
    for half in range(2):
        rows = [cw[0], cw[1], cw[2], cb] if half == 0 else [cw[2], cw[1], cw[0], cb]
        a = np.stack(rows, axis=1).reshape(2 * FC, 128, 4).transpose(1, 0, 2)
        convp.append(f32c(a))
    in_maps = []
    for c in range(8):
        b, half = c // 2, c % 2
        xwin = x[b, 0:TOK] if half == 0 else x[b, 4096 - TOK:4096][::-1]
        m = dict(shared)
        m["xw"] = f32c(xwin)
        m["nab"] = nab[half]
        m["convp"] = convp[half]
        in_maps.append(m)
    key = tuple(_debug_outs)
    if key not in _NC_CACHE:
        _NC_CACHE[key] = build(_debug_outs)
    nc = _NC_CACHE[key]
    res = run_bass_kernel_spmd(nc, in_maps, core_ids=list(range(8)))
    out = np.empty((4, 4096, D), np.float32)
    for c in range(8):
        b, half = c // 2, c % 2
        yl = res.results[c]["y"]
        if half == 0:
            out[b, 0:2048] = yl
        else:
            out[b, 2048:4096] = yl[::-1]
    if _debug_outs:
        return out, res.results
    return out
```
